# Optimizing a Trainium2 kernel written in Bass

```python
import jax, jax.numpy as jnp
from jax import lax
import numpy as np


D_MODEL = 1024
BATCH = 8
SEQ = 4096
DEPTH = 2

N_A_LAYERS = DEPTH // 2
N_B_LAYERS = DEPTH - N_A_LAYERS
HG_EXPAND = 128
HG_HEADS = D_MODEL // HG_EXPAND
HG_DV = D_MODEL // HG_HEADS
HG_CHUNK = 32
ATT_HEAD_DIM = 64
ATT_Q_HEADS = D_MODEL // ATT_HEAD_DIM
ATT_KV_HEADS = 2
ATT_GROUP = ATT_Q_HEADS // ATT_KV_HEADS
WINDOW = 128
D_FF = 2816
CONV_WIDTH = 3
EPS = 1e-6

kernel_name = 'yoco_hgrn2_swa_sink_alibi_convffn'

F32 = jnp.float32


def rms_norm(x, g):
    xf = x.astype(F32)
    xf = xf * lax.rsqrt(jnp.mean(xf * xf, axis=-1, keepdims=True) + EPS)
    return (xf * g.astype(F32)).astype(x.dtype)


def alibi_slopes(n_heads):
    return jnp.asarray(2.0 ** (-8.0 * np.arange(1, n_heads + 1) / n_heads), F32)


def hgrn2_chunked(q, k, v, logf):
    b_, s_, h_, dk = q.shape
    dv = v.shape[-1]
    n_chunks = s_ // HG_CHUNK

    def to_chunks(t):
        return t.reshape(b_, n_chunks, HG_CHUNK, h_, t.shape[-1]).transpose(1, 0, 3, 2, 4)

    qc, kc, vc, gc = to_chunks(q), to_chunks(k), to_chunks(v), to_chunks(logf)
    causal = jnp.tril(jnp.ones((HG_CHUNK, HG_CHUNK), bool))[:, :, None]

    def step(state, inp):
        qb, kb, vb, gb = inp
        cum = jnp.cumsum(gb, axis=2)
        o_inter = jnp.einsum('bhtk,bhkv->bhtv', qb * jnp.exp(cum), state)
        rel = cum[:, :, :, None, :] - cum[:, :, None, :, :]
        decay = jnp.exp(jnp.where(causal, rel, -jnp.inf))
        scores = jnp.einsum('bhtk,bhsk,bhtsk->bhts', qb, kb, decay)
        o_intra = jnp.einsum('bhts,bhsv->bhtv', scores, vb)
        last = cum[:, :, -1:, :]
        new_state = (jnp.exp(last[:, :, 0, :])[..., None] * state
                     + jnp.einsum('bhsk,bhsv->bhkv', kb * jnp.exp(last - cum), vb))
        return new_state, o_inter + o_intra

    s0 = jnp.zeros((b_, h_, dk, dv), F32)
    _, o = lax.scan(step, s0, (qc, kc, vc, gc))
    return o.transpose(1, 0, 3, 2, 4).reshape(b_, s_, h_, dv)


def hgrn2_mixer(x, w_in, lower_bound, out_norm, w_out):
    b_, s_, _ = x.shape
    q, f, i, g = jnp.split(x @ w_in, 4, axis=-1)
    q = jax.nn.silu(q.astype(F32)) * HG_EXPAND ** -0.5
    forget = lower_bound + (1.0 - lower_bound) * jax.nn.sigmoid(f.astype(F32))
    logf = jnp.log(forget)
    k = 1.0 - forget
    heads = lambda t: t.reshape(b_, s_, HG_HEADS, -1)
    o = hgrn2_chunked(heads(q), heads(k), heads(i.astype(F32)), heads(logf))
    o = rms_norm(o, out_norm) * jax.nn.silu(heads(g.astype(F32)))
    return o.reshape(b_, s_, D_MODEL).astype(x.dtype) @ w_out


def shared_kv(h, kv_norm, w_kv):
    b_, s_, _ = h.shape
    k, v = jnp.split(rms_norm(h, kv_norm) @ w_kv, 2, axis=-1)
    return (k.reshape(b_, s_, ATT_KV_HEADS, ATT_HEAD_DIM),
            v.reshape(b_, s_, ATT_KV_HEADS, ATT_HEAD_DIM))


def swa_sink_attention(x, k, v, w_q, sinks, w_o):
    b_, s_, _ = x.shape
    nb = s_ // WINDOW
    q = (x @ w_q).reshape(b_, nb, WINDOW, ATT_KV_HEADS, ATT_GROUP, ATT_HEAD_DIM)

    def band(t):
        tb = t.reshape(b_, nb, WINDOW, ATT_KV_HEADS, ATT_HEAD_DIM)
        prev = jnp.pad(tb[:, :-1], ((0, 0), (1, 0), (0, 0), (0, 0), (0, 0)))
        return jnp.concatenate([prev, tb], axis=2)

    kb, vb = band(k), band(v)
    scores = jnp.einsum('bnqkgd,bnskd->bnkgqs', q.astype(F32), kb.astype(F32)) * ATT_HEAD_DIM ** -0.5
    q_idx = jnp.arange(WINDOW)[:, None] + WINDOW
    k_idx = jnp.arange(2 * WINDOW)[None, :]
    dist = q_idx - k_idx
    key_abs = (jnp.arange(nb) * WINDOW)[:, None] + jnp.arange(2 * WINDOW)[None, :] - WINDOW
    valid = ((dist >= 0) & (dist < WINDOW))[None] & (key_abs >= 0)[:, None, :]
    slopes = alibi_slopes(ATT_Q_HEADS).reshape(ATT_KV_HEADS, ATT_GROUP)
    scores = scores - slopes[:, :, None, None] * dist.astype(F32)
    scores = jnp.where(valid[None, :, None, None], scores, -jnp.inf)
    sink = sinks.astype(F32).reshape(ATT_KV_HEADS, ATT_GROUP)[None, None, :, :, None, None]
    m = jnp.maximum(jnp.max(scores, axis=-1, keepdims=True), sink)
    e = jnp.exp(scores - m)
    probs = e / (jnp.sum(e, axis=-1, keepdims=True) + jnp.exp(sink - m))
    out = jnp.einsum('bnkgqs,bnskd->bnqkgd', probs, vb.astype(F32))
    return out.reshape(b_, s_, ATT_Q_HEADS * ATT_HEAD_DIM).astype(x.dtype) @ w_o


def conv_ffn(x, w_up, conv_w, conv_b, w_down):
    s_ = x.shape[1]
    gate, val = jnp.split(x @ w_up, 2, axis=-1)
    gp = jnp.pad(gate, ((0, 0), (CONV_WIDTH - 1, 0), (0, 0)))
    conv = conv_b
    for j in range(CONV_WIDTH):
        conv = conv + conv_w[j] * gp[:, j:j + s_]
    return (jax.nn.silu(conv) * val) @ w_down


def setup_inputs(seed: int = 0) -> dict:
    key = jax.random.key(seed)
    ks = jax.random.split(key, 18)
    D = D_MODEL
    HQD = ATT_Q_HEADS * ATT_HEAD_DIM
    KVD = ATT_KV_HEADS * ATT_HEAD_DIM

    def w(k, shape, fan_in):
        return jax.random.normal(k, shape, F32) * fan_in ** -0.5

    def gain(k, shape):
        return 1.0 + 0.02 * jax.random.normal(k, shape, F32)

    return {
        'x': jax.random.normal(ks[0], (BATCH, SEQ, D), F32),
        'hg_norm': gain(ks[1], (N_A_LAYERS, D)),
        'hg_w_in': w(ks[2], (N_A_LAYERS, D, 4 * D), D),
        'hg_lb_logits': 0.1 * jax.random.normal(ks[3], (N_A_LAYERS + 1, D), F32),
        'hg_out_norm': gain(ks[4], (N_A_LAYERS, HG_DV)),
        'hg_w_out': w(ks[5], (N_A_LAYERS, D, D), D),
        'kv_norm': gain(ks[6], (D,)),
        'w_kv': w(ks[7], (D, 2 * KVD), D),
        'attn_norm': gain(ks[8], (N_B_LAYERS, D)),
        'attn_w_q': w(ks[9], (N_B_LAYERS, D, HQD), D),
        'attn_sinks': 0.5 * jax.random.normal(ks[10], (N_B_LAYERS, ATT_Q_HEADS), F32),
        'attn_w_o': w(ks[11], (N_B_LAYERS, HQD, D), HQD),
        'ffn_norm': gain(ks[12], (DEPTH, D)),
        'ffn_w_up': w(ks[13], (DEPTH, D, 2 * D_FF), D),
        'ffn_conv_w': w(ks[14], (DEPTH, CONV_WIDTH, D_FF), CONV_WIDTH),
        'ffn_conv_b': 0.02 * jax.random.normal(ks[15], (DEPTH, D_FF), F32),
        'ffn_w_down': w(ks[16], (DEPTH, D_FF, D), D_FF),
        'final_norm': gain(ks[17], (D,)),
    }


def reference(x, hg_norm, hg_w_in, hg_lb_logits, hg_out_norm, hg_w_out, kv_norm, w_kv,
              attn_norm, attn_w_q, attn_sinks, attn_w_o, ffn_norm, ffn_w_up, ffn_conv_w,
              ffn_conv_b, ffn_w_down, final_norm):
    lower_bounds = jnp.cumsum(jax.nn.softmax(hg_lb_logits.astype(F32), axis=0), axis=0)
    h = x
    k_sh, v_sh = None, None
    for layer in range(DEPTH):
        if layer < N_A_LAYERS:
            a = layer
            h = h + hgrn2_mixer(rms_norm(h, hg_norm[a]), hg_w_in[a], lower_bounds[a],
                                hg_out_norm[a], hg_w_out[a])
        else:
            bi = layer - N_A_LAYERS
            if bi == 0:
                k_sh, v_sh = shared_kv(h, kv_norm, w_kv)
            h = h + swa_sink_attention(rms_norm(h, attn_norm[bi]), k_sh, v_sh,
                                       attn_w_q[bi], attn_sinks[bi], attn_w_o[bi])
        h = h + conv_ffn(rms_norm(h, ffn_norm[layer]), ffn_w_up[layer], ffn_conv_w[layer],
                         ffn_conv_b[layer], ffn_w_down[layer])
    return rms_norm(h, final_norm)
```

```python
import contextlib
import math
import numpy as np
import concourse.bass as bass
import concourse.mybir as mybir
from concourse.bass_utils import run_bass_kernel_spmd

F32 = mybir.dt.float32
BF16 = mybir.dt.bfloat16
AF = mybir.ActivationFunctionType
ALU = mybir.AluOpType

D = 1024
T = 512
KC = 8
NH = 8
DFF = 2816
NFC = 22
SEQ = 4096
EPS = 1e-6
NV = 257
NTB = 512
RING = 18432
PAGE = 1024
NWSEM = 8
LOOK = 3
USE_SCRATCH = True
HOIST = True
HG_LEVEL = 9
DB2 = 3
CHAIN = 1
DELTA = 1
ENGS = ("pe", "dve", "act", "pool", "sp")


class V:
    __slots__ = ("ap", "res")

    def __init__(self, ap, res):
        self.ap = ap
        self.res = tuple(res)


class Sched:
    def __init__(self, nc):
        self.nc = nc
        self.ops = {e: [] for e in ENGS}
        self.last_w = {}
        self.readers = {}
        self.known = {e: {} for e in ENGS}
        self.dma_cnt = {}
        self.nseq = {e: 0 for e in ENGS}
        self.out_tokens = []
        self.gidx = 0
        self.tok_idx = {}

    def _deps(self, eng, reads, writes):
        deps = {}

        def add(tok, same_ok):
            key, val, src = tok
            if src == eng and not same_ok:
                return
            if deps.get(key, 0) < val:
                deps[key] = val

        for r in reads:
            t = self.last_w.get(r)
            if t is not None:
                add(t, eng != "pe")
        for w in writes:
            t = self.last_w.get(w)
            if t is not None:
                add(t, eng != "pe")
            for key, (val, src) in self.readers.get(w, {}).items():
                add((key, val, src), eng != "pe")
        waits = []
        kn = self.known[eng]
        for key, val in deps.items():
            if kn.get(key, 0) < val:
                kn[key] = val
                waits.append((key, val))
        return waits

    def _commit(self, tok, reads, writes):
        key, val, src = tok
        for r in reads:
            d = self.readers.setdefault(r, {})
            if d.get(key, (0, None))[0] < val:
                d[key] = (val, src)
        for w in writes:
            self.last_w[w] = tok
            self.readers[w] = {}

    @staticmethod
    def _split(args, kw):
        reads, writes = [], []
        a2 = []
        for i, a in enumerate(args):
            if isinstance(a, V):
                (writes if i == 0 else reads).extend(a.res)
                a2.append(a.ap)
            else:
                a2.append(a)
        k2 = {}
        for k, a in kw.items():
            if isinstance(a, V):
                (writes if k in ("out", "accum_out") else reads).extend(a.res)
                k2[k] = a.ap
            else:
                k2[k] = a
        return a2, k2, reads, writes

    def op(self, eng, method, *args, xr=(), xw=(), **kw):
        a2, k2, reads, writes = self._split(args, kw)
        reads += list(xr)
        writes += list(xw)
        waits = self._deps(eng, reads, writes)
        self.nseq[eng] += 1
        tok = (eng, self.nseq[eng], eng)
        self._commit(tok, reads, writes)
        self.gidx += 1
        self.tok_idx[(eng, self.nseq[eng])] = self.gidx
        self.ops[eng].append((waits, method, a2, k2, (eng, 1), self.gidx))
        return tok

    def dma(self, eng, out, in_, semkey, is_output=False, xw=(), **kw):
        a2, k2, reads, writes = self._split((out, in_), kw)
        writes += list(xw) + [("sem", semkey)]
        waits = self._deps(eng, reads, writes)
        self.dma_cnt[semkey] = self.dma_cnt.get(semkey, 0) + 16
        tok = (semkey, self.dma_cnt[semkey], "dma")
        self._commit(tok, reads, writes)
        self.gidx += 1
        self.tok_idx[(semkey, self.dma_cnt[semkey])] = self.gidx
        self.ops[eng].append((waits, "dma_start", a2, k2, (semkey, 16), self.gidx))
        if is_output:
            self.out_tokens.append(tok)
        return tok

    def emit(self):
        nc = self.nc
        keys = set(ENGS) | set(self.dma_cnt.keys())
        keys = sorted(keys, key=str)
        with contextlib.ExitStack() as es:
            sems = {}
            for i, k in enumerate(keys):
                sems[k] = es.enter_context(nc.semaphore("s%d" % i))
            block = es.enter_context(nc.Block())
            fin = {}
            for key, val, _ in self.out_tokens:
                fin[key] = max(fin.get(key, 0), val)

            def plan(name):
                ops = self.ops[name]
                att = [None] * len(ops)
                alone = [[] for _ in ops]
                for i, (waits, method, a, k, inc, gi) in enumerate(ops):
                    if not waits:
                        continue
                    if method == "dma_start" or not HOIST:
                        alone[i] = list(waits)
                        continue
                    ws = sorted(waits, key=lambda w: -self.tok_idx.get((w[0], w[1]), 0))
                    att[i] = ws[0]
                    for w in ws[1:]:
                        ti = self.tok_idx.get((w[0], w[1]), 0)
                        placed = False
                        for i2 in range(i - 1, max(i - 24, -1), -1):
                            if ops[i2][5] <= ti:
                                break
                            if att[i2] is None and ops[i2][1] != "dma_start" and not alone[i2]:
                                att[i2] = w
                                placed = True
                                break
                        if not placed:
                            alone[i].append(w)
                return att, alone

            def run(engine, name):
                att, alone = plan(name)
                for i, (waits, method, a, k, (skey, amt), gi) in enumerate(self.ops[name]):
                    for wk, wv in alone[i]:
                        engine.wait_ge(sems[wk], wv)
                    inst = getattr(engine, method)(*a, **k)
                    if att[i] is not None:
                        inst._wait_ge(sems[att[i][0]], att[i][1])
                    inst.then_inc(sems[skey], amt)
                if name == "sp":
                    for key, val in fin.items():
                        engine.wait_ge(sems[key], val)

            @block.tensor
            def _(e):
                run(e, "pe")

            @block.vector
            def _(e):
                run(e, "dve")

            @block.scalar
            def _(e):
                run(e, "act")

            @block.gpsimd
            def _(e):
                run(e, "pool")

            @block.sync
            def _(e):
                run(e, "sp")


class Buf:
    def __init__(self, nc, name, shape, dtype, axis=1, g=None, psum=False):
        if psum:
            self.t = nc.alloc_psum_tensor("t_" + name, list(shape), dtype)
        else:
            self.t = nc.alloc_sbuf_tensor("t_" + name, list(shape), dtype)
        self.name = name
        self.shape = tuple(shape)
        self.axis = axis
        self.g = g if g is not None else shape[axis]

    def res(self, idx):
        if not isinstance(idx, tuple):
            idx = (idx,)
        n = self.shape[self.axis]
        if self.axis < len(idx):
            s = idx[self.axis]
            if isinstance(s, int):
                lo, hi = s, s + 1
            else:
                lo = 0 if s.start is None else s.start
                hi = n if s.stop is None else s.stop
        else:
            lo, hi = 0, n
        return [(self.name, i) for i in range(lo // self.g, (hi - 1) // self.g + 1)]

    def __getitem__(self, idx):
        return V(self.t[idx], self.res(idx))

    def allres(self):
        return self.res(())


def alibi_slopes():
    return [float(np.float32(2.0 ** (-8.0 * (h + 1) / 16))) for h in range(16)]


def build(NT=8, dump=False, phases=("hgrn", "ffn0", "attn", "ffn1")):
    S_ = NT * T
    nc = bass.Bass("TRN2", target_bir_lowering=False)
    S = Sched(nc)

    def din(name, shape):
        return nc.dram_tensor(name, list(shape), F32, kind="ExternalInput").ap()

    xT_d = din("xT", [KC, 128, S_])
    wi_d = din("w_i", [128, 8 * 1024])
    win_d = din("w_in", [NH, 128, 8 * 384])
    wout_d = din("w_out", [128, 8 * 1024])
    wup_d = din("w_up", [2, NFC, 128, 8 * 256])
    wdn_d = din("w_down", [2, KC, 128, NFC * 128])
    wkv_d = din("w_kv", [128, 8 * 512])
    wq_d = din("w_q", [128, 8 * 1024])
    wo_d = din("w_o", [128, 8 * 1024])
    vecs_d = din("vecs", [128, NV])
    tabs_d = din("tabs", [128, NTB])
    yT_d = nc.dram_tensor("yT", [KC, 128, S_], F32, kind="ExternalOutput").ap()
    dbg_d = None
    if dump:
        dbg_d = nc.dram_tensor("dbg", [4, KC, 128, S_], F32, kind="ExternalOutput").ap()

    B = lambda *a, **k: Buf(nc, *a, **k)
    hT = B("hT", [128, KC, T], F32, axis=1, g=1)
    xn = B("xn", [128, KC, T], BF16, axis=1, g=1)
    xsq = B("xsq", [128, 2, T], BF16, axis=1, g=1)
    rs = B("rs", [128, T], F32)
    rv = B("rv", [128, T], F32)
    cst = B("cst", [128, 8], F32)
    vtm = B("vtm", [128, 4, 1024], BF16, axis=1, g=1)
    mo = B("mo", [128, KC, T], BF16, axis=1, g=1)
    qs = B("qs", [128, T], F32)
    gsb = B("gsb", [128, 3, T], F32, axis=1, g=1)
    th = B("th", [128, T], F32)
    fg = B("fg", [128, T], F32)
    kk = B("kk", [128, T], F32)
    cp = B("cp", [128, T], F32)
    ri = B("ri", [128, T], F32)
    d1 = B("d1", [128, T], F32)
    qt = B("qt", [128, 3, T], BF16, axis=1, g=1)
    kt = B("kt", [128, 3, T], BF16, axis=1, g=1)
    kh = B("kh", [128, 3, T], BF16, axis=1, g=1)
    khT = B("khT", [128, 2, T], BF16, axis=1, g=1)
    Am = B("Am", [128, 2, T], BF16, axis=1, g=1)
    eb = B("eb", [128, 3, 8], F32, axis=1, g=1)
    osq = B("osq", [128, T], BF16)
    t1 = B("t1", [128, T], F32)
    S32 = B("S32", [128, NH, 128], F32, axis=1, g=1)
    SbP = B("SbP", [128, 2 * NH, 128], BF16, axis=1, g=1)
    Rr = B("Rr", [128, 16, 128], BF16, axis=1, g=1)
    hbuf = B("hbuf", [128, NFC, T], BF16, axis=1, g=1)
    G = B("G", [128, 2, T + 2], F32, axis=1, g=1)
    acc = B("acc", [128, 2, T], F32, axis=1, g=1)
    sl = B("sl", [128, 2, T], F32, axis=1, g=1)
    halo = B("halo", [128, 2 * NFC, 2], F32, axis=1, g=1)
    qT = B("qT", [128, KC, T], BF16, axis=1, g=1)
    kT = B("kT", [128, 2, 5 * 128], BF16, axis=2, g=128)
    vd = B("vd", [128, 5, 256], BF16, axis=1, g=1)
    scb = B("scb", [128, 2, 1024], F32, axis=1, g=1)
    pbuf = B("pbuf", [128, 2, 1024], BF16, axis=1, g=1)
    rec = B("rec", [128, 2, T], F32, axis=1, g=1)
    ost = B("ost", [128, 2, T], F32, axis=1, g=1)
    vecs = B("vecs", [128, NV], F32)
    tabs = B("tabs", [128, NTB], F32)
    identb = B("identb", [128, 128], BF16)
    onesb = B("onesb", [128, 128], BF16)
    lbt = B("lbt", [128, 8], F32)
    Ac = B("Ac", [128, 8], F32)
    Bc = B("Bc", [128, 8], F32)
    nBc = B("nBc", [128, 8], F32)
    esink = B("esink", [128, 16], F32)
    WR = B("WR", [128, RING], BF16, axis=1, g=PAGE)
    ps = [B("ps%d" % i, [128, 512], F32, axis=1, g=512, psum=True) for i in range(8)]

    slopes = alibi_slopes()

    wsched = []

    scr = {}

    def sdram(name, shape):
        return nc.dram_tensor(name, list(shape), BF16).ap()

    wi_s = sdram("s_w_i", [128, 8 * 1024])
    win_s = sdram("s_w_in", [NH, 128, 8 * 384])
    wout_s = sdram("s_w_out", [128, 8 * 1024])
    wup_s = sdram("s_w_up", [2, NFC, 128, 8 * 256])
    wdn_s = sdram("s_w_down", [2, KC, 128, NFC * 128])
    wkv_s = sdram("s_w_kv", [128, 8 * 512])
    wq_s = sdram("s_w_q", [128, 8 * 1024])
    wo_s = sdram("s_w_o", [128, 8 * 1024])
    scr_of = {id(wi_d): wi_s, id(wout_d): wout_s, id(wkv_d): wkv_s, id(wq_d): wq_s, id(wo_d): wo_s}

    def wadd(key, ap, n, sap=None):
        wsched.append((key, ap, n, sap))

    for j in range(NT):
        if "hgrn" in phases:
            wadd(("win", j, 0), win_d[0], 3072, win_s[0])
            wadd(("wi", j), wi_d, 8192, wi_s)
            for h in range(1, NH):
                wadd(("win", j, h), win_d[h], 3072, win_s[h])
            wadd(("wout", j), wout_d, 8192, wout_s)
        for l in range(2):
            if l == 1 and "attn" in phases:
                wadd(("wkv", j), wkv_d, 4096, wkv_s)
                wadd(("wq", j), wq_d, 8192, wq_s)
                wadd(("wo", j), wo_d, 8192, wo_s)
            if ("ffn%d" % l) in phases:
                for c in range(NFC):
                    wadd(("wup", j, l, c), wup_d[l, c], 2048, wup_s[l, c])
                for oc in range(KC):
                    wadd(("wdn", j, l, oc), wdn_d[l, oc], 2816, wdn_s[l, oc])
    widx = {k: i for i, (k, _, _, _) in enumerate(wsched)}
    wstate = {"next": 0, "head": 0, "live": [], "views": {}, "cnt": 0}

    def _try_alloc(n):
        live = wstate["live"]
        head = wstate["head"]
        if not live:
            wstate["head"] = n
            return 0
        tail = live[0][1]
        if head >= tail:
            if head + n <= RING:
                wstate["head"] = head + n
                return head
            if n < tail:
                wstate["head"] = n
                return 0
            return None
        if head + n < tail:
            wstate["head"] = head + n
            return head
        return None

    def _emit_load(i):
        key, ap, n, sap = wsched[i]
        off = _try_alloc(n)
        if off is None:
            return False
        wstate["live"].append((key, off, off + n))
        semkey = ("dw", wstate["cnt"] % NWSEM)
        wstate["cnt"] += 1
        jt = key[1]
        sres = [("scr",) + (key[0],) + tuple(key[2:])]
        if jt == 0 or not USE_SCRATCH:
            S.dma("pool", WR[:, off:off + n], V(ap, []), semkey)
            if USE_SCRATCH and NT > 1:
                S.dma("sp", V(sap, sres), WR[:, off:off + n], ("db", wstate["cnt"] % 4))
        else:
            S.dma("pool", WR[:, off:off + n], V(sap, sres), semkey)
        wstate["views"][key] = (off, n)
        return True

    def wget(key, inner):
        i = widx[key]
        while wstate["next"] <= min(i + LOOK, len(wsched) - 1):
            if not _emit_load(wstate["next"]):
                break
            wstate["next"] += 1
        assert key in wstate["views"], ("ring too small for", key)
        off, n = wstate["views"][key]
        res = WR.res((slice(None), slice(off, off + n)))
        view = WR.t[:, off:off + n].rearrange("p (k c) -> p k c", c=inner)
        return view, res

    def wdone(key):
        k0 = wstate["live"].pop(0)
        assert k0[0] == key, (k0, key)
        del wstate["views"][key]

    def vcol(c):
        return vecs[:, c:c + 1]

    def gain(gi, kc):
        return vcol(gi * 8 + kc)

    def cw(l, jj, c):
        return vcol(48 + (l * 3 + jj) * NFC + c)

    def cb(l, c):
        return vcol(180 + l * NFC + c)

    def stats_sq(kc):
        S.op("act", "activation", xsq[:, kc % 2, :], hT[:, kc, :], AF.Square)

    def stats_mm(kc):
        S.op("pe", "matmul", ps[7][:, :], onesb[:, :], xsq[:, kc % 2, :], start=(kc == 0), stop=(kc == KC - 1))

    def stats_fin():
        S.op("act", "activation", rv[:, :], ps[7][:, :], AF.Ln, bias=cst[:, 0:1], scale=1.0 / D)
        S.op("act", "activation", rs[:, :], rv[:, :], AF.Exp, scale=-0.5)

    def rms_stats():
        for kc in range(KC):
            stats_sq(kc)
            stats_mm(kc)
        stats_fin()

    def rms_apply(gi, dst=None):
        dst = xn if dst is None else dst
        for kc in range(KC):
            S.op("dve", "scalar_tensor_tensor", dst[:, kc, :], hT[:, kc, :], gain(gi, kc), rs[:, :], ALU.mult, ALU.mult)

    def proj_residual(key, src):
        wv, wres = wget(key, 1024)
        for oc in range(KC):
            pb = ps[oc % 4]
            for kc in range(KC):
                S.op("pe", "matmul", pb[:, :], V(wv[:, kc, oc * 128:(oc + 1) * 128], wres), src[:, kc, :],
                     start=(kc == 0), stop=(kc == KC - 1))
            if oc >= 1:
                stats_mm(oc - 1)
            S.op("dve", "tensor_tensor", hT[:, oc, :], hT[:, oc, :], pb[:, :], ALU.add)
            stats_sq(oc)
        stats_mm(KC - 1)
        stats_fin()
        wdone(key)

    def dump_h(j, stage):
        if dbg_d is None:
            return
        for kc in range(KC):
            S.dma("sp", V(dbg_d[stage, kc, :, j * T:(j + 1) * T], []), hT[:, kc, :], ("dd", kc), is_output=True)

    S.dma("sp", vecs[:, :], V(vecs_d, []), ("ds", 0))
    S.dma("sp", tabs[:, :], V(tabs_d, []), ("ds", 1))
    S.op("pool", "memset", onesb[:, :], 1.0)
    S.op("pool", "memset", cst[:, :], EPS)
    S.op("pool", "memset", halo[:, :, :], 0.0)
    S.op("pool", "memset", S32[:, :, :], 0.0)
    S.op("pool", "memset", SbP[:, :, :], 0.0)
    S.op("pool", "memset", kT[:, :, :], 0.0)
    S.op("pool", "memset", vd[:, :, :], 0.0)
    S.op("pool", "memset", d1[:, :], 0.0)
    S.op("dve", "tensor_copy", identb[:, :], tabs[:, 0:128])
    S.op("dve", "tensor_tensor", lbt[:, :], vecs[:, 225:233], vecs[:, 233:241], ALU.subtract)
    S.op("act", "activation", lbt[:, :], lbt[:, :], AF.Tanh, scale=0.5)
    S.op("dve", "tensor_scalar", Ac[:, :], lbt[:, :], 0.25, 0.75, ALU.mult, ALU.add)
    S.op("dve", "tensor_scalar", Bc[:, :], lbt[:, :], -0.25, 0.25, ALU.mult, ALU.add)
    S.op("dve", "tensor_scalar", nBc[:, :], lbt[:, :], 0.25, -0.25, ALU.mult, ALU.add)
    S.op("act", "activation", esink[:, :], vecs[:, 241:257], AF.Exp)

    amask = V(tabs.t[:, 128:256].rearrange("p (o c) -> p o c", o=1).to_broadcast([128, 4, 128]), tabs.allres())
    dtab = V(tabs.t[:, 256:512].rearrange("p (o c) -> p o c", o=1).to_broadcast([128, 2, 256]), tabs.allres())

    rs2 = B("rs2", [128, T], F32)

    def hgrn_iter(j, sX, sY, sZ):
        if sX is not None:
            hX, pX = sX, sX % 3
            wv, wres = wget(("win", j, hX), 384)
        if sY is not None:
            hY, pY3, pY2 = sY, sY % 3, sY % 2
            wrY = ((j + 1) % 2) * NH + hY
            pt3 = ps[3].t[:, :].bitcast(BF16)
            khTv = khT.t[:, pY2, :].rearrange("p (b k) -> p b k", k=128)
            AmvY = Am.t[:, pY2, :].rearrange("p (b k) -> p b k", k=128)
        if sZ is not None:
            hZ, pZ3, pZ2 = sZ, sZ % 3, sZ % 2
            rdZ = (j % 2) * NH + hZ
            AmvZ = Am.t[:, pZ2, :].rearrange("p (b k) -> p b k", k=128)
        if sX is not None:
            order = [(idx, kc) for idx in range(3) for kc in range(KC)]
            if hX == 0:
                order = [(idx, kc) for kc in range(KC) for idx in range(3)]
            for idx, kc in order:
                S.op("pe", "matmul", ps[idx][:, :], V(wv[:, kc, idx * 128:(idx + 1) * 128], wres), xn[:, kc, :],
                     start=(kc == 0), stop=(kc == KC - 1))
            wdone(("win", j, hX))
        if sY is not None:
            for pr in range(4):
                S.op("pe", "transpose", V(pt3[:, pr * 128:(pr + 1) * 128], ps[3].allres()), kh[:, pY3, pr * 128:(pr + 1) * 128], identb[:, :])
        if sZ is not None:
            for pr in range(4):
                S.op("pe", "matmul", ps[6][:, pr * 128:(pr + 1) * 128], vtm[:, pr, hZ * 128:(hZ + 1) * 128],
                     V(AmvZ[:, pr, :], Am.res((slice(None), pZ2))), start=True, stop=False)
                for half in range(2):
                    c = 2 * pr + half
                    sprev = SbP[:, rdZ, :] if c == 0 else Rr[:, pZ2 * 8 + c - 1, :]
                    S.op("pe", "matmul", ps[6][:, c * 64:(c + 1) * 64], sprev, qt[:, pZ3, c * 64:(c + 1) * 64],
                         start=False, stop=(half == 1))
        if sX is not None:
            S.op("act", "activation", qs[:, :], ps[0][:, :], AF.Silu)
            S.op("act", "activation", gsb[:, pX, :], ps[1][:, :], AF.Silu)
            S.op("act", "activation", th[:, :], ps[2][:, :], AF.Tanh, scale=0.5)
            S.op("pool", "tensor_scalar", fg[:, :], th[:, :], Bc[:, hX:hX + 1], Ac[:, hX:hX + 1], ALU.mult, ALU.add)
            S.op("pool", "tensor_scalar", kk[:, :], th[:, :], nBc[:, hX:hX + 1], Bc[:, hX:hX + 1], ALU.mult, ALU.add)
        if sY is not None:
            S.op("act", "activation", khT[:, pY2, :], V(pt3[:, 0:512], ps[3].allres()), AF.Copy)
        if sZ is not None:
            S.op("act", "activation", osq[:, :], ps[6][:, :], AF.Square)
        if sX is not None:
            fgv = fg.t[:, :].rearrange("p (c j) -> p c j", j=64)
            d1v = d1.t[:, :].rearrange("p (c j) -> p c j", j=64)
            S.op("dve", "tensor_copy", V(d1v[:, :, 0:1], d1.allres()), V(fgv[:, :, 0:1], fg.allres()))
            S.op("dve", "memset", V(fgv[:, :, 0:1], fg.allres()), 0.0)
            S.op("dve", "tensor_tensor_scan", cp[:, :], fg[:, :], d1[:, :], 0.0, ALU.mult, ALU.add)
        if sY is not None:
            for c in range(8):
                blk, half = c // 2, c % 2
                dbank = ps[4] if c % 2 == 0 else ps[3]
                S.op("pe", "matmul", dbank[:, (c // 2) * 128:(c // 2 + 1) * 128],
                     V(khTv[half * 64:(half + 1) * 64, blk, :], khT.res((slice(None), pY2))),
                     vtm[half * 64:(half + 1) * 64, blk, hY * 128:(hY + 1) * 128], start=True, stop=True)
            for pr in range(4):
                S.op("pe", "matmul", ps[5][:, pr * 128:(pr + 1) * 128], kt[:, pY3, pr * 128:(pr + 1) * 128],
                     qt[:, pY3, pr * 128:(pr + 1) * 128], start=True, stop=True)
        if sZ is not None:
            S.op("pe", "matmul", ps[7][:, :], onesb[:, :], osq[:, :], start=True, stop=True)
        if sX is not None:
            S.op("act", "activation", ri[:, :], cp[:, :], AF.Ln)
            S.op("act", "activation", ri[:, :], ri[:, :], AF.Exp, scale=-1.0)
            S.op("dve", "scalar_tensor_tensor", qt[:, pX, :], qs[:, :], 128 ** -0.5, cp[:, :], ALU.mult, ALU.mult)
            cpv = cp.t[:, :].rearrange("p (c j) -> p c j", j=64)
            S.op("dve", "tensor_copy", V(eb.t[:, pX, :].rearrange("p (c o) -> p c o", o=1), eb.res((slice(None), pX))),
                 V(cpv[:, :, 63:64], cp.allres()))
        if sY is not None:
            S.op("dve", "tensor_tensor", V(AmvY, Am.res((slice(None), pY2))),
                 V(ps[5].t[:, :].rearrange("p (b k) -> p b k", k=128), ps[5].allres()), amask, ALU.mult)
            for c in range(8):
                dbank = ps[4] if c % 2 == 0 else ps[3]
                S.op("dve", "scalar_tensor_tensor", S32[:, hY, :], S32[:, hY, :], eb[:, pY3, c:c + 1],
                     dbank[:, (c // 2) * 128:(c // 2 + 1) * 128], ALU.mult, ALU.add)
                dst = Rr[:, pY2 * 8 + c, :] if c < 7 else SbP[:, wrY, :]
                S.op("dve", "tensor_copy", dst, S32[:, hY, :])
        if sZ is not None:
            S.op("act", "activation", rs2[:, :], ps[7][:, :], AF.Ln, bias=cst[:, 0:1], scale=1.0 / 128)
            S.op("act", "activation", rs2[:, :], rs2[:, :], AF.Exp, scale=-0.5)
        if sX is not None:
            S.op("dve", "tensor_tensor", kt[:, pX, :], kk[:, :], ri[:, :], ALU.mult)
            ktv = kt.t[:, pX, :].rearrange("p (c j) -> p c j", j=64)
            khv = kh.t[:, pX, :].rearrange("p (c j) -> p c j", j=64)
            S.op("dve", "tensor_tensor", V(khv, kh.res((slice(None), pX))), V(ktv, kt.res((slice(None), pX))),
                 V(cpv[:, :, 63:64].to_broadcast([128, 8, 64]), cp.allres()), ALU.mult)
        if sZ is not None:
            S.op("dve", "tensor_tensor", t1[:, :], ps[6][:, :], rs2[:, :], ALU.mult)
            S.op("dve", "scalar_tensor_tensor", mo[:, hZ, :], t1[:, :], vcol(224), gsb[:, pZ3, :], ALU.mult, ALU.mult)

    def hgrn_layer(j):
        rms_apply(0)
        hgrn_iter(j, 0, None, None)
        wv, wres = wget(("wi", j), 1024)
        for blk in range(4):
            for half in range(2):
                n = blk * 2 + half
                pb = ps[3 + n % 2]
                for kc in range(KC):
                    S.op("pe", "matmul", pb[:, :], xn[:, kc, blk * 128:(blk + 1) * 128],
                         V(wv[:, kc, half * 512:(half + 1) * 512], wres), start=(kc == 0), stop=(kc == KC - 1))
                S.op("act", "activation", vtm[:, blk, half * 512:(half + 1) * 512], pb[:, :], AF.Copy)
        wdone(("wi", j))
        for step in range(1, NH + 2):
            hgrn_iter(j, step if step < NH else None, step - 1 if 0 <= step - 1 < NH else None,
                      step - 2 if 0 <= step - 2 < NH else None)
        proj_residual(("wout", j), mo)

    def ffn(j, l):
        rms_apply(1 if l == 0 else 4)

        def stage1_pe(cs):
            ws = [wget(("wup", j, l, c), 256) for c in cs]
            if len(cs) == 1:
                order = [(0, half, kc) for half in range(2) for kc in range(KC)]
            else:
                order = [(i, half, kc) for kc in range(KC) for i in range(len(cs)) for half in range(2)]
            for i, half, kc in order:
                c = cs[i]
                wv, wres = ws[i]
                pb = ps[c % 2] if half == 0 else ps[2 + c % 5]
                S.op("pe", "matmul", pb[:, :], V(wv[:, kc, half * 128:(half + 1) * 128], wres), xn[:, kc, :],
                     start=(kc == 0), stop=(kc == KC - 1))
            for c in cs:
                wdone(("wup", j, l, c))

        def stage1(c, pe=True):
            if pe:
                stage1_pe([c])
            b = c % 2
            pg, pv = ps[b], ps[2 + c % 5]
            hl = l * NFC + c
            S.op("dve", "tensor_copy", G[:, b, 0:2], halo[:, hl, :])
            S.op("act", "activation", G[:, b, 2:T + 2], pg[:, :], AF.Copy)
            S.op("act", "activation", acc[:, b, :], pg[:, :], AF.Identity, bias=cb(l, c), scale=cw(l, 2, c))
            S.op("dve", "tensor_copy", halo[:, hl, :], G[:, b, T:T + 2])

        def stage2(c):
            b = c % 2
            pv = ps[2 + c % 5]
            S.op("dve", "scalar_tensor_tensor", acc[:, b, :], G[:, b, 1:T + 1], cw(l, 1, c), acc[:, b, :], ALU.mult, ALU.add)
            S.op("dve", "scalar_tensor_tensor", acc[:, b, :], G[:, b, 0:T], cw(l, 0, c), acc[:, b, :], ALU.mult, ALU.add)
            S.op("act", "activation", sl[:, b, :], acc[:, b, :], AF.Silu)
            S.op("dve", "tensor_tensor", hbuf[:, c, :], sl[:, b, :], pv[:, :], ALU.mult)

        stage1_pe([0, 1])
        stage1(0, pe=False)
        stage1(1, pe=False)
        stage2(0)
        for c in range(2, NFC + 1):
            if c < NFC:
                stage1(c)
            stage2(c - 1)
        for oc in range(KC):
            wv, wres = wget(("wdn", j, l, oc), 128)
            pb = ps[oc % 2]
            for c in range(NFC):
                S.op("pe", "matmul", pb[:, :], V(wv[:, c, :], wres), hbuf[:, c, :], start=(c == 0), stop=(c == NFC - 1))
            if oc >= 1:
                stats_mm(oc - 1)
            S.op("dve", "tensor_tensor", hT[:, oc, :], hT[:, oc, :], pb[:, :], ALU.add)
            stats_sq(oc)
            wdone(("wdn", j, l, oc))
        stats_mm(KC - 1)
        stats_fin()

    def attn_layer(j):
        rms_apply(2, dst=mo)
        rms_apply(3)
        wv, wres = wget(("wkv", j), 512)
        for g in range(2):
            for kc in range(KC):
                S.op("pe", "matmul", ps[g][:, :], V(wv[:, kc, g * 128:(g + 1) * 128], wres), mo[:, kc, :],
                     start=(kc == 0), stop=(kc == KC - 1))
            S.op("act", "activation", kT[:, g, 128:640], ps[g][:, :], AF.Copy)
        for blk in range(4):
            pb = ps[2 + blk % 2]
            for kc in range(KC):
                S.op("pe", "matmul", pb[:, 0:256], mo[:, kc, blk * 128:(blk + 1) * 128], V(wv[:, kc, 256:512], wres),
                     start=(kc == 0), stop=(kc == KC - 1))
            S.op("act", "activation", vd[:, 1 + blk, :], pb[:, 0:256], AF.Copy)
        wdone(("wkv", j))
        wv, wres = wget(("wq", j), 1024)
        for qc in range(KC):
            pb = ps[4 + qc % 2]
            for kc in range(KC):
                S.op("pe", "matmul", pb[:, :], V(wv[:, kc, qc * 128:(qc + 1) * 128], wres), xn[:, kc, :],
                     start=(kc == 0), stop=(kc == KC - 1))
            S.op("act", "activation", qT[:, qc, :], pb[:, :], AF.Copy)
        wdone(("wq", j))
        dtab_r = [V(tabs.t[:, 256 + r * 128:256 + (r + 1) * 128].rearrange("p (o c) -> p o c", o=1).to_broadcast([128, 4, 128]),
                    tabs.allres()) for r in range(2)]

        def s1(hh):
            g, qc, po, b = hh // 8, hh // 2, (hh % 2) * 64, hh % 2
            for r in range(2):
                bank = ps[2 * b + r]
                for qb in range(4):
                    slot = qb + r
                    S.op("pe", "matmul", bank[:, qb * 128:(qb + 1) * 128],
                         kT[po:po + 64, g, slot * 128:(slot + 1) * 128], qT[po:po + 64, qc, qb * 128:(qb + 1) * 128],
                         start=True, stop=True)
            for r in range(2):
                bank = ps[2 * b + r]
                S.op("dve", "scalar_tensor_tensor",
                     V(scb.t[:, b, r * 512:(r + 1) * 512].rearrange("p (q c) -> p q c", c=128), scb.res((slice(None), b))),
                     dtab_r[r], -8.0 * slopes[hh],
                     V(bank.t[:, :].rearrange("p (q c) -> p q c", c=128), bank.allres()), ALU.mult, ALU.add)
            S.op("act", "activation", pbuf[:, b, :], scb[:, b, :], AF.Exp, scale=0.125)

        def s2(hh):
            g, qc, po, b = hh // 8, hh // 2, (hh % 2) * 64, hh % 2
            PV, DN = ps[4 + b], ps[6 + b]
            pres = pbuf.res((slice(None), b))
            for qb in range(4):
                first = True
                for r in range(2):
                    if j == 0 and qb == 0 and r == 0:
                        continue
                    slot = qb + r
                    n = r * 4 + qb
                    S.op("pe", "matmul", PV[:, qb * 128:(qb + 1) * 128], vd[:, slot, g * 128:(g + 1) * 128],
                         pbuf[:, b, n * 128:(n + 1) * 128], start=first, stop=(r == 1))
                    first = False
            S.op("pe", "matmul", DN[:, :], onesb[:, :], pbuf[:, b, 512:1024], start=True, stop=False)
            if j == 0:
                S.op("pe", "matmul", DN[:, 128:512], onesb[:, :], pbuf[:, b, 128:512], start=False, stop=True)
            else:
                S.op("pe", "matmul", DN[:, :], onesb[:, :], pbuf[:, b, 0:512], start=False, stop=True)
            S.op("act", "activation", rec[po:po + 64, b, :], DN[po:po + 64, :], AF.Ln, bias=esink[po:po + 64, hh:hh + 1])
            S.op("act", "activation", rec[po:po + 64, b, :], rec[po:po + 64, b, :], AF.Exp, scale=-1.0)
            S.op("dve", "tensor_tensor", mo[po:po + 64, qc, :], PV[po:po + 64, :], rec[po:po + 64, b, :], ALU.mult)

        s1(0)
        for hh in range(16):
            if hh + 1 < 16:
                s1(hh + 1)
            s2(hh)
        for g in range(2):
            S.op("act", "activation", kT[:, g, 0:128], kT[:, g, 512:640], AF.Copy)
        S.op("act", "activation", vd[:, 0, :], vd[:, 4, :], AF.Copy)
        proj_residual(("wo", j), mo)

    for j in range(NT):
        for kc in range(KC):
            S.dma("sp", hT[:, kc, :], V(xT_d[kc, :, j * T:(j + 1) * T], []), ("dx", kc))
        rms_stats()
        if "hgrn" in phases:
            hgrn_layer(j)
        dump_h(j, 0)
        if "ffn0" in phases:
            ffn(j, 0)
        dump_h(j, 1)
        if "attn" in phases:
            attn_layer(j)
        dump_h(j, 2)
        if "ffn1" in phases:
            ffn(j, 1)
        dump_h(j, 3)
        for kc in range(KC):
            b = kc % 2
            S.op("dve", "scalar_tensor_tensor", ost[:, b, :], hT[:, kc, :], gain(5, kc), rs[:, :], ALU.mult, ALU.mult)
            S.dma("sp", V(yT_d[kc, :, j * T:(j + 1) * T], []), ost[:, b, :], ("dy", b), is_output=True)
    S.emit()
    return nc


def _chunked(w):
    n = w.shape[1]
    return np.ascontiguousarray(w.reshape(KC, 128, n).transpose(1, 0, 2)).reshape(128, KC * n)


def host_layout(inputs, NT=8):
    f = lambda a: np.ascontiguousarray(np.asarray(a, dtype=np.float32))
    hg_w_in = f(inputs["hg_w_in"])[0]
    out = {}
    out["w_i"] = _chunked(hg_w_in[:, 2048:3072])
    w_in = np.empty((NH, 128, KC * 384), np.float32)
    for h in range(NH):
        cols = np.concatenate([hg_w_in[:, h * 128:(h + 1) * 128],
                               hg_w_in[:, 3072 + h * 128:3072 + (h + 1) * 128],
                               hg_w_in[:, 1024 + h * 128:1024 + (h + 1) * 128]], axis=1)
        w_in[h] = _chunked(cols)
    out["w_in"] = w_in
    out["w_out"] = _chunked(f(inputs["hg_w_out"])[0])
    w_up_in = f(inputs["ffn_w_up"])
    w_up = np.empty((2, NFC, 128, KC * 256), np.float32)
    for l in range(2):
        for c in range(NFC):
            cols = np.concatenate([w_up_in[l][:, c * 128:(c + 1) * 128],
                                   w_up_in[l][:, DFF + c * 128:DFF + (c + 1) * 128]], axis=1)
            w_up[l, c] = _chunked(cols)
    out["w_up"] = w_up
    w_dn_in = f(inputs["ffn_w_down"])
    w_dn = np.empty((2, KC, 128, NFC * 128), np.float32)
    for l in range(2):
        wd = w_dn_in[l].reshape(NFC, 128, KC, 128)
        w_dn[l] = wd.transpose(2, 1, 0, 3).reshape(KC, 128, NFC * 128)
    out["w_down"] = w_dn
    w_kv = f(inputs["w_kv"])
    k0, k1, v0, v1 = w_kv[:, 0:64], w_kv[:, 64:128], w_kv[:, 128:192], w_kv[:, 192:256]
    out["w_kv"] = _chunked(np.concatenate([k0, k0, k1, k1, v0, v0, v1, v1], axis=1))
    out["w_q"] = _chunked(f(inputs["attn_w_q"])[0])
    out["w_o"] = _chunked(f(inputs["attn_w_o"])[0])
    vecs = np.zeros((128, NV), np.float32)
    gl = [inputs["hg_norm"][0], inputs["ffn_norm"][0], inputs["kv_norm"], inputs["attn_norm"][0],
          inputs["ffn_norm"][1], inputs["final_norm"]]
    for gi, gvec in enumerate(gl):
        vecs[:, gi * 8:(gi + 1) * 8] = f(gvec).reshape(KC, 128).T
    cwv = f(inputs["ffn_conv_w"])
    cbv = f(inputs["ffn_conv_b"])
    for l in range(2):
        for jj in range(3):
            vecs[:, 48 + (l * 3 + jj) * NFC:48 + (l * 3 + jj + 1) * NFC] = cwv[l, jj].reshape(NFC, 128).T
        vecs[:, 180 + l * NFC:180 + (l + 1) * NFC] = cbv[l].reshape(NFC, 128).T
    vecs[:, 224] = f(inputs["hg_out_norm"])[0]
    lbl = f(inputs["hg_lb_logits"])
    for r in range(2):
        vecs[:, 225 + r * 8:225 + (r + 1) * 8] = lbl[r].reshape(NH, 128).T
    vecs[:, 241:257] = f(inputs["attn_sinks"])[0][None, :]
    out["vecs"] = vecs
    tabs = np.zeros((128, NTB), np.float32)
    tabs[:, 0:128] = np.eye(128, dtype=np.float32)
    s = np.arange(128)[:, None]
    t = np.arange(128)[None, :]
    tabs[:, 128:256] = ((s // 64 == t // 64) & (s <= t)).astype(np.float32)
    BIG = 1.0e6
    tabs[:, 256:384] = np.where(t < s, 128.0 + t - s, BIG)
    tabs[:, 384:512] = np.where(t >= s, (t - s).astype(np.float64), BIG)
    out["tabs"] = tabs
    return out


_NC_CACHE = {}


def kernel(**inputs):
    x = np.asarray(inputs["x"], dtype=np.float32)
    nb, s_, d_ = x.shape
    NT = s_ // T
    shared = host_layout(inputs, NT)
    if NT not in _NC_CACHE:
        _NC_CACHE[NT] = build(NT)
    nc = _NC_CACHE[NT]
    in_maps = []
    for b in range(nb):
        m = dict(shared)
        m["xT"] = np.ascontiguousarray(x[b].T).reshape(KC, 128, s_)
        in_maps.append(m)
    res = run_bass_kernel_spmd(nc, in_maps, core_ids=list(range(nb)))
    out = np.empty((nb, s_, d_), np.float32)
    for b in range(nb):
        out[b] = res.results[b]["yT"].reshape(D, s_).T
    return out
```

```python
import contextlib
import math
import numpy as np
import concourse.bass as bass
import concourse.mybir as mybir
from concourse.bass_utils import run_bass_kernel_spmd

F32 = mybir.dt.float32
BF16 = mybir.dt.bfloat16
AF = mybir.ActivationFunctionType
ALU = mybir.AluOpType

D = 1024
T = 512
KC = 8
NH = 8
DFF = 2816
NFC = 22
SEQ = 4096
EPS = 1e-6
NV = 257
NTB = 512
RING = 18432
PAGE = 1024
NWSEM = 8
LOOK = 3
USE_SCRATCH = True
HOIST = True
VCLOCK = True
HG_LEVEL = 9
DB2 = 3
CHAIN = 1
DELTA = 1
ENGS = ("pe", "dve", "act", "pool", "sp")


class V:
    __slots__ = ("ap", "res")

    def __init__(self, ap, res):
        self.ap = ap
        self.res = tuple(res)


class Sched:
    def __init__(self, nc):
        self.nc = nc
        self.ops = {e: [] for e in ENGS}
        self.last_w = {}
        self.readers = {}
        self.known = {e: {} for e in ENGS}
        self.dma_cnt = {}
        self.nseq = {e: 0 for e in ENGS}
        self.out_tokens = []
        self.gidx = 0
        self.tok_idx = {}
        self.clock = {}

    def _deps(self, eng, reads, writes):
        deps = {}

        def add(tok, same_ok):
            key, val, src = tok
            if src == eng and not same_ok:
                return
            if deps.get(key, 0) < val:
                deps[key] = val

        for r in reads:
            t = self.last_w.get(r)
            if t is not None:
                add(t, eng != "pe")
        for w in writes:
            t = self.last_w.get(w)
            if t is not None:
                add(t, eng != "pe")
            for key, (val, src) in self.readers.get(w, {}).items():
                add((key, val, src), eng != "pe")
        waits = []
        kn = self.known[eng]
        for key, val in sorted(deps.items(), key=lambda kv: -self.tok_idx.get(kv, 0)):
            if kn.get(key, 0) < val:
                waits.append((key, val))
                if VCLOCK:
                    for k2, v2 in self.clock.get((key, val), {}).items():
                        if kn.get(k2, 0) < v2:
                            kn[k2] = v2
                kn[key] = max(kn.get(key, 0), val)
        return waits

    def _commit(self, tok, reads, writes):
        key, val, src = tok
        for r in reads:
            d = self.readers.setdefault(r, {})
            if d.get(key, (0, None))[0] < val:
                d[key] = (val, src)
        for w in writes:
            self.last_w[w] = tok
            self.readers[w] = {}

    @staticmethod
    def _split(args, kw):
        reads, writes = [], []
        a2 = []
        for i, a in enumerate(args):
            if isinstance(a, V):
                (writes if i == 0 else reads).extend(a.res)
                a2.append(a.ap)
            else:
                a2.append(a)
        k2 = {}
        for k, a in kw.items():
            if isinstance(a, V):
                (writes if k in ("out", "accum_out") else reads).extend(a.res)
                k2[k] = a.ap
            else:
                k2[k] = a
        return a2, k2, reads, writes

    def op(self, eng, method, *args, xr=(), xw=(), **kw):
        a2, k2, reads, writes = self._split(args, kw)
        reads += list(xr)
        writes += list(xw)
        waits = self._deps(eng, reads, writes)
        self.nseq[eng] += 1
        tok = (eng, self.nseq[eng], eng)
        self._commit(tok, reads, writes)
        self.gidx += 1
        self.tok_idx[(eng, self.nseq[eng])] = self.gidx
        ck = dict(self.known[eng])
        ck[eng] = self.nseq[eng]
        self.clock[(eng, self.nseq[eng])] = ck
        self.ops[eng].append((waits, method, a2, k2, (eng, 1), self.gidx))
        return tok

    def dma(self, eng, out, in_, semkey, is_output=False, xw=(), **kw):
        a2, k2, reads, writes = self._split((out, in_), kw)
        writes += list(xw) + [("sem", semkey)]
        waits = self._deps(eng, reads, writes)
        self.dma_cnt[semkey] = self.dma_cnt.get(semkey, 0) + 16
        tok = (semkey, self.dma_cnt[semkey], "dma")
        self._commit(tok, reads, writes)
        self.gidx += 1
        self.tok_idx[(semkey, self.dma_cnt[semkey])] = self.gidx
        ck = dict(self.known[eng])
        ck[semkey] = self.dma_cnt[semkey]
        self.clock[(semkey, self.dma_cnt[semkey])] = ck
        self.ops[eng].append((waits, "dma_start", a2, k2, (semkey, 16), self.gidx))
        if is_output:
            self.out_tokens.append(tok)
        return tok

    def emit(self):
        nc = self.nc
        keys = set(ENGS) | set(self.dma_cnt.keys())
        keys = sorted(keys, key=str)
        with contextlib.ExitStack() as es:
            sems = {}
            for i, k in enumerate(keys):
                sems[k] = es.enter_context(nc.semaphore("s%d" % i))
            block = es.enter_context(nc.Block())
            fin = {}
            for key, val, _ in self.out_tokens:
                fin[key] = max(fin.get(key, 0), val)

            def plan(name):
                ops = self.ops[name]
                att = [None] * len(ops)
                alone = [[] for _ in ops]
                for i, (waits, method, a, k, inc, gi) in enumerate(ops):
                    if not waits:
                        continue
                    if method == "dma_start" or not HOIST:
                        alone[i] = list(waits)
                        continue
                    ws = sorted(waits, key=lambda w: -self.tok_idx.get((w[0], w[1]), 0))
                    att[i] = ws[0]
                    for w in ws[1:]:
                        ti = self.tok_idx.get((w[0], w[1]), 0)
                        placed = False
                        for i2 in range(i - 1, max(i - 24, -1), -1):
                            if ops[i2][5] <= ti:
                                break
                            if att[i2] is None and ops[i2][1] != "dma_start" and not alone[i2]:
                                att[i2] = w
                                placed = True
                                break
                        if not placed:
                            alone[i].append(w)
                return att, alone

            def run(engine, name):
                att, alone = plan(name)
                for i, (waits, method, a, k, (skey, amt), gi) in enumerate(self.ops[name]):
                    for wk, wv in alone[i]:
                        engine.wait_ge(sems[wk], wv)
                    inst = getattr(engine, method)(*a, **k)
                    if att[i] is not None:
                        inst._wait_ge(sems[att[i][0]], att[i][1])
                    inst.then_inc(sems[skey], amt)
                if name == "sp":
                    for key, val in fin.items():
                        engine.wait_ge(sems[key], val)

            @block.tensor
            def _(e):
                run(e, "pe")

            @block.vector
            def _(e):
                run(e, "dve")

            @block.scalar
            def _(e):
                run(e, "act")

            @block.gpsimd
            def _(e):
                run(e, "pool")

            @block.sync
            def _(e):
                run(e, "sp")


class Buf:
    def __init__(self, nc, name, shape, dtype, axis=1, g=None, psum=False):
        if psum:
            self.t = nc.alloc_psum_tensor("t_" + name, list(shape), dtype)
        else:
            self.t = nc.alloc_sbuf_tensor("t_" + name, list(shape), dtype)
        self.name = name
        self.shape = tuple(shape)
        self.axis = axis
        self.g = g if g is not None else shape[axis]

    def res(self, idx):
        if not isinstance(idx, tuple):
            idx = (idx,)
        n = self.shape[self.axis]
        if self.axis < len(idx):
            s = idx[self.axis]
            if isinstance(s, int):
                lo, hi = s, s + 1
            else:
                lo = 0 if s.start is None else s.start
                hi = n if s.stop is None else s.stop
        else:
            lo, hi = 0, n
        return [(self.name, i) for i in range(lo // self.g, (hi - 1) // self.g + 1)]

    def __getitem__(self, idx):
        return V(self.t[idx], self.res(idx))

    def allres(self):
        return self.res(())


def alibi_slopes():
    return [float(np.float32(2.0 ** (-8.0 * (h + 1) / 16))) for h in range(16)]


def build(NT=8, dump=False, phases=("hgrn", "ffn0", "attn", "ffn1")):
    S_ = NT * T
    nc = bass.Bass("TRN2", target_bir_lowering=False)
    S = Sched(nc)

    def din(name, shape):
        return nc.dram_tensor(name, list(shape), F32, kind="ExternalInput").ap()

    xT_d = din("xT", [KC, 128, S_])
    wi_d = din("w_i", [128, 8 * 1024])
    win_d = din("w_in", [NH, 128, 8 * 384])
    wout_d = din("w_out", [128, 8 * 1024])
    wup_d = din("w_up", [2, NFC, 128, 8 * 256])
    wdn_d = din("w_down", [2, KC, 128, NFC * 128])
    wkv_d = din("w_kv", [128, 8 * 512])
    wq_d = din("w_q", [128, 8 * 1024])
    wo_d = din("w_o", [128, 8 * 1024])
    vecs_d = din("vecs", [128, NV])
    tabs_d = din("tabs", [128, NTB])
    yT_d = nc.dram_tensor("yT", [KC, 128, S_], F32, kind="ExternalOutput").ap()
    dbg_d = None
    if dump:
        dbg_d = nc.dram_tensor("dbg", [4, KC, 128, S_], F32, kind="ExternalOutput").ap()

    B = lambda *a, **k: Buf(nc, *a, **k)
    hT = B("hT", [128, KC, T], F32, axis=1, g=1)
    xn = B("xn", [128, KC, T], BF16, axis=1, g=1)
    xsq = B("xsq", [128, 2, T], BF16, axis=1, g=1)
    rs = B("rs", [128, T], F32)
    rv = B("rv", [128, T], F32)
    cst = B("cst", [128, 8], F32)
    vtm = B("vtm", [128, 4, 1024], BF16, axis=1, g=1)
    mo = B("mo", [128, KC, T], BF16, axis=1, g=1)
    qs = B("qs", [128, T], F32)
    gsb = B("gsb", [128, 3, T], F32, axis=1, g=1)
    th = B("th", [128, T], F32)
    fg = B("fg", [128, T], F32)
    kk = B("kk", [128, T], F32)
    cp = B("cp", [128, T], F32)
    ri = B("ri", [128, T], F32)
    d1 = B("d1", [128, T], F32)
    qt = B("qt", [128, 3, T], BF16, axis=1, g=1)
    kt = B("kt", [128, 3, T], BF16, axis=1, g=1)
    kh = B("kh", [128, 3, T], BF16, axis=1, g=1)
    khT = B("khT", [128, 2, T], BF16, axis=1, g=1)
    Am = B("Am", [128, 2, T], BF16, axis=1, g=1)
    eb = B("eb", [128, 3, 8], F32, axis=1, g=1)
    osq = B("osq", [128, T], BF16)
    t1 = B("t1", [128, T], F32)
    S32 = B("S32", [128, NH, 128], F32, axis=1, g=1)
    SbP = B("SbP", [128, 2 * NH, 128], BF16, axis=1, g=1)
    Rr = B("Rr", [128, 16, 128], BF16, axis=1, g=1)
    hbuf = B("hbuf", [128, NFC, T], BF16, axis=1, g=1)
    G = B("G", [128, 2, T + 2], F32, axis=1, g=1)
    acc = B("acc", [128, 2, T], F32, axis=1, g=1)
    sl = B("sl", [128, 2, T], F32, axis=1, g=1)
    halo = B("halo", [128, 2 * NFC, 2], F32, axis=1, g=1)
    qT = B("qT", [128, KC, T], BF16, axis=1, g=1)
    kT = B("kT", [128, 2, 5 * 128], BF16, axis=2, g=128)
    vd = B("vd", [128, 5, 256], BF16, axis=1, g=1)
    scb = B("scb", [128, 2, 1024], F32, axis=1, g=1)
    pbuf = B("pbuf", [128, 2, 1024], BF16, axis=1, g=1)
    rec = B("rec", [128, 2, T], F32, axis=1, g=1)
    ost = B("ost", [128, 2, T], F32, axis=1, g=1)
    vecs = B("vecs", [128, NV], F32)
    tabs = B("tabs", [128, NTB], F32)
    identb = B("identb", [128, 128], BF16)
    onesb = B("onesb", [128, 128], BF16)
    lbt = B("lbt", [128, 8], F32)
    Ac = B("Ac", [128, 8], F32)
    Bc = B("Bc", [128, 8], F32)
    nBc = B("nBc", [128, 8], F32)
    esink = B("esink", [128, 16], F32)
    WR = B("WR", [128, RING], BF16, axis=1, g=PAGE)
    ps = [B("ps%d" % i, [128, 512], F32, axis=1, g=512, psum=True) for i in range(8)]

    slopes = alibi_slopes()

    wsched = []

    scr = {}

    def sdram(name, shape):
        return nc.dram_tensor(name, list(shape), BF16).ap()

    wi_s = sdram("s_w_i", [128, 8 * 1024])
    win_s = sdram("s_w_in", [NH, 128, 8 * 384])
    wout_s = sdram("s_w_out", [128, 8 * 1024])
    wup_s = sdram("s_w_up", [2, NFC, 128, 8 * 256])
    wdn_s = sdram("s_w_down", [2, KC, 128, NFC * 128])
    wkv_s = sdram("s_w_kv", [128, 8 * 512])
    wq_s = sdram("s_w_q", [128, 8 * 1024])
    wo_s = sdram("s_w_o", [128, 8 * 1024])
    scr_of = {id(wi_d): wi_s, id(wout_d): wout_s, id(wkv_d): wkv_s, id(wq_d): wq_s, id(wo_d): wo_s}

    def wadd(key, ap, n, sap=None):
        wsched.append((key, ap, n, sap))

    for j in range(NT):
        if "hgrn" in phases:
            wadd(("win", j, 0), win_d[0], 3072, win_s[0])
            wadd(("wi", j), wi_d, 8192, wi_s)
            for h in range(1, NH):
                wadd(("win", j, h), win_d[h], 3072, win_s[h])
            wadd(("wout", j), wout_d, 8192, wout_s)
        for l in range(2):
            if l == 1 and "attn" in phases:
                wadd(("wkv", j), wkv_d, 4096, wkv_s)
                wadd(("wq", j), wq_d, 8192, wq_s)
                wadd(("wo", j), wo_d, 8192, wo_s)
            if ("ffn%d" % l) in phases:
                for c in range(NFC):
                    wadd(("wup", j, l, c), wup_d[l, c], 2048, wup_s[l, c])
                for oc in range(KC):
                    wadd(("wdn", j, l, oc), wdn_d[l, oc], 2816, wdn_s[l, oc])
    widx = {k: i for i, (k, _, _, _) in enumerate(wsched)}
    wstate = {"next": 0, "head": 0, "live": [], "views": {}, "cnt": 0}

    def _try_alloc(n):
        live = wstate["live"]
        head = wstate["head"]
        if not live:
            wstate["head"] = n
            return 0
        tail = live[0][1]
        if head >= tail:
            if head + n <= RING:
                wstate["head"] = head + n
                return head
            if n < tail:
                wstate["head"] = n
                return 0
            return None
        if head + n < tail:
            wstate["head"] = head + n
            return head
        return None

    def _emit_load(i):
        key, ap, n, sap = wsched[i]
        off = _try_alloc(n)
        if off is None:
            return False
        wstate["live"].append((key, off, off + n))
        semkey = ("dw", wstate["cnt"] % NWSEM)
        wstate["cnt"] += 1
        jt = key[1]
        sres = [("scr",) + (key[0],) + tuple(key[2:])]
        if jt == 0 or not USE_SCRATCH:
            S.dma("pool", WR[:, off:off + n], V(ap, []), semkey)
            if USE_SCRATCH and NT > 1:
                S.dma("sp", V(sap, sres), WR[:, off:off + n], ("db", wstate["cnt"] % 4))
        else:
            S.dma("pool", WR[:, off:off + n], V(sap, sres), semkey)
        wstate["views"][key] = (off, n)
        return True

    def wget(key, inner):
        i = widx[key]
        while wstate["next"] <= min(i + LOOK, len(wsched) - 1):
            if not _emit_load(wstate["next"]):
                break
            wstate["next"] += 1
        assert key in wstate["views"], ("ring too small for", key)
        off, n = wstate["views"][key]
        res = WR.res((slice(None), slice(off, off + n)))
        view = WR.t[:, off:off + n].rearrange("p (k c) -> p k c", c=inner)
        return view, res

    def wdone(key):
        k0 = wstate["live"].pop(0)
        assert k0[0] == key, (k0, key)
        del wstate["views"][key]

    def vcol(c):
        return vecs[:, c:c + 1]

    def gain(gi, kc):
        return vcol(gi * 8 + kc)

    def cw(l, jj, c):
        return vcol(48 + (l * 3 + jj) * NFC + c)

    def cb(l, c):
        return vcol(180 + l * NFC + c)

    def stats_sq(kc):
        S.op("act", "activation", xsq[:, kc % 2, :], hT[:, kc, :], AF.Square)

    def stats_mm(kc):
        S.op("pe", "matmul", ps[7][:, :], onesb[:, :], xsq[:, kc % 2, :], start=(kc == 0), stop=(kc == KC - 1))

    def stats_fin():
        S.op("act", "activation", rv[:, :], ps[7][:, :], AF.Ln, bias=cst[:, 0:1], scale=1.0 / D)
        S.op("act", "activation", rs[:, :], rv[:, :], AF.Exp, scale=-0.5)

    def rms_stats():
        for kc in range(KC):
            stats_sq(kc)
            stats_mm(kc)
        stats_fin()

    def rms_apply(gi, dst=None):
        dst = xn if dst is None else dst
        for kc in range(KC):
            S.op("dve", "scalar_tensor_tensor", dst[:, kc, :], hT[:, kc, :], gain(gi, kc), rs[:, :], ALU.mult, ALU.mult)

    def proj_residual(key, src):
        wv, wres = wget(key, 1024)
        for oc in range(KC):
            pb = ps[oc % 4]
            for kc in range(KC):
                S.op("pe", "matmul", pb[:, :], V(wv[:, kc, oc * 128:(oc + 1) * 128], wres), src[:, kc, :],
                     start=(kc == 0), stop=(kc == KC - 1))
            if oc >= 1:
                stats_mm(oc - 1)
            S.op("dve", "tensor_tensor", hT[:, oc, :], hT[:, oc, :], pb[:, :], ALU.add)
            stats_sq(oc)
        stats_mm(KC - 1)
        stats_fin()
        wdone(key)

    def dump_h(j, stage):
        if dbg_d is None:
            return
        for kc in range(KC):
            S.dma("sp", V(dbg_d[stage, kc, :, j * T:(j + 1) * T], []), hT[:, kc, :], ("dd", kc), is_output=True)

    S.dma("sp", vecs[:, :], V(vecs_d, []), ("ds", 0))
    S.dma("sp", tabs[:, :], V(tabs_d, []), ("ds", 1))
    S.op("pool", "memset", onesb[:, :], 1.0)
    S.op("pool", "memset", cst[:, :], EPS)
    S.op("pool", "memset", halo[:, :, :], 0.0)
    S.op("pool", "memset", S32[:, :, :], 0.0)
    S.op("pool", "memset", SbP[:, :, :], 0.0)
    S.op("pool", "memset", kT[:, :, :], 0.0)
    S.op("pool", "memset", vd[:, :, :], 0.0)
    S.op("pool", "memset", d1[:, :], 0.0)
    S.op("dve", "tensor_copy", identb[:, :], tabs[:, 0:128])
    S.op("dve", "tensor_tensor", lbt[:, :], vecs[:, 225:233], vecs[:, 233:241], ALU.subtract)
    S.op("act", "activation", lbt[:, :], lbt[:, :], AF.Tanh, scale=0.5)
    S.op("dve", "tensor_scalar", Ac[:, :], lbt[:, :], 0.25, 0.75, ALU.mult, ALU.add)
    S.op("dve", "tensor_scalar", Bc[:, :], lbt[:, :], -0.25, 0.25, ALU.mult, ALU.add)
    S.op("dve", "tensor_scalar", nBc[:, :], lbt[:, :], 0.25, -0.25, ALU.mult, ALU.add)
    S.op("act", "activation", esink[:, :], vecs[:, 241:257], AF.Exp)

    amask = V(tabs.t[:, 128:256].rearrange("p (o c) -> p o c", o=1).to_broadcast([128, 4, 128]), tabs.allres())
    dtab = V(tabs.t[:, 256:512].rearrange("p (o c) -> p o c", o=1).to_broadcast([128, 2, 256]), tabs.allres())

    rs2 = B("rs2", [128, T], F32)

    def hgrn_iter(j, sX, sY, sZ):
        if sX is not None:
            hX, pX = sX, sX % 3
            wv, wres = wget(("win", j, hX), 384)
        if sY is not None:
            hY, pY3, pY2 = sY, sY % 3, sY % 2
            wrY = ((j + 1) % 2) * NH + hY
            pt3 = ps[3].t[:, :].bitcast(BF16)
            khTv = khT.t[:, pY2, :].rearrange("p (b k) -> p b k", k=128)
            AmvY = Am.t[:, pY2, :].rearrange("p (b k) -> p b k", k=128)
        if sZ is not None:
            hZ, pZ3, pZ2 = sZ, sZ % 3, sZ % 2
            rdZ = (j % 2) * NH + hZ
            AmvZ = Am.t[:, pZ2, :].rearrange("p (b k) -> p b k", k=128)
        if sX is not None:
            order = [(idx, kc) for idx in range(3) for kc in range(KC)]
            if hX == 0:
                order = [(idx, kc) for kc in range(KC) for idx in range(3)]
            for idx, kc in order:
                S.op("pe", "matmul", ps[idx][:, :], V(wv[:, kc, idx * 128:(idx + 1) * 128], wres), xn[:, kc, :],
                     start=(kc == 0), stop=(kc == KC - 1))
            wdone(("win", j, hX))
        if sY is not None:
            for pr in range(4):
                S.op("pe", "transpose", V(pt3[:, pr * 128:(pr + 1) * 128], ps[3].allres()), kh[:, pY3, pr * 128:(pr + 1) * 128], identb[:, :])
        if sZ is not None:
            for pr in range(4):
                S.op("pe", "matmul", ps[6][:, pr * 128:(pr + 1) * 128], vtm[:, pr, hZ * 128:(hZ + 1) * 128],
                     V(AmvZ[:, pr, :], Am.res((slice(None), pZ2))), start=True, stop=False)
                for half in range(2):
                    c = 2 * pr + half
                    sprev = SbP[:, rdZ, :] if c == 0 else Rr[:, pZ2 * 8 + c - 1, :]
                    S.op("pe", "matmul", ps[6][:, c * 64:(c + 1) * 64], sprev, qt[:, pZ3, c * 64:(c + 1) * 64],
                         start=False, stop=(half == 1))
        if sX is not None:
            S.op("act", "activation", qs[:, :], ps[0][:, :], AF.Silu)
            S.op("act", "activation", gsb[:, pX, :], ps[1][:, :], AF.Silu)
            S.op("act", "activation", th[:, :], ps[2][:, :], AF.Tanh, scale=0.5)
            S.op("pool", "tensor_scalar", fg[:, :], th[:, :], Bc[:, hX:hX + 1], Ac[:, hX:hX + 1], ALU.mult, ALU.add)
            S.op("pool", "tensor_scalar", kk[:, :], th[:, :], nBc[:, hX:hX + 1], Bc[:, hX:hX + 1], ALU.mult, ALU.add)
        if sY is not None:
            S.op("act", "activation", khT[:, pY2, :], V(pt3[:, 0:512], ps[3].allres()), AF.Copy)
        if sZ is not None:
            S.op("act", "activation", osq[:, :], ps[6][:, :], AF.Square)
        if sX is not None:
            fgv = fg.t[:, :].rearrange("p (c j) -> p c j", j=64)
            d1v = d1.t[:, :].rearrange("p (c j) -> p c j", j=64)
            S.op("dve", "tensor_copy", V(d1v[:, :, 0:1], d1.allres()), V(fgv[:, :, 0:1], fg.allres()))
            S.op("dve", "memset", V(fgv[:, :, 0:1], fg.allres()), 0.0)
            S.op("dve", "tensor_tensor_scan", cp[:, :], fg[:, :], d1[:, :], 0.0, ALU.mult, ALU.add)
        if sY is not None:
            for c in range(8):
                blk, half = c // 2, c % 2
                dbank = ps[4] if c % 2 == 0 else ps[3]
                S.op("pe", "matmul", dbank[:, (c // 2) * 128:(c // 2 + 1) * 128],
                     V(khTv[half * 64:(half + 1) * 64, blk, :], khT.res((slice(None), pY2))),
                     vtm[half * 64:(half + 1) * 64, blk, hY * 128:(hY + 1) * 128], start=True, stop=True)
            for pr in range(4):
                S.op("pe", "matmul", ps[5][:, pr * 128:(pr + 1) * 128], kt[:, pY3, pr * 128:(pr + 1) * 128],
                     qt[:, pY3, pr * 128:(pr + 1) * 128], start=True, stop=True)
        if sZ is not None:
            S.op("pe", "matmul", ps[7][:, :], onesb[:, :], osq[:, :], start=True, stop=True)
        if sX is not None:
            S.op("act", "activation", ri[:, :], cp[:, :], AF.Ln)
            S.op("act", "activation", ri[:, :], ri[:, :], AF.Exp, scale=-1.0)
            S.op("dve", "scalar_tensor_tensor", qt[:, pX, :], qs[:, :], 128 ** -0.5, cp[:, :], ALU.mult, ALU.mult)
            cpv = cp.t[:, :].rearrange("p (c j) -> p c j", j=64)
            S.op("dve", "tensor_copy", V(eb.t[:, pX, :].rearrange("p (c o) -> p c o", o=1), eb.res((slice(None), pX))),
                 V(cpv[:, :, 63:64], cp.allres()))
        if sY is not None:
            S.op("dve", "tensor_tensor", V(AmvY, Am.res((slice(None), pY2))),
                 V(ps[5].t[:, :].rearrange("p (b k) -> p b k", k=128), ps[5].allres()), amask, ALU.mult)
            for c in range(8):
                dbank = ps[4] if c % 2 == 0 else ps[3]
                S.op("dve", "scalar_tensor_tensor", S32[:, hY, :], S32[:, hY, :], eb[:, pY3, c:c + 1],
                     dbank[:, (c // 2) * 128:(c // 2 + 1) * 128], ALU.mult, ALU.add)
                dst = Rr[:, pY2 * 8 + c, :] if c < 7 else SbP[:, wrY, :]
                S.op("dve", "tensor_copy", dst, S32[:, hY, :])
        if sZ is not None:
            S.op("act", "activation", rs2[:, :], ps[7][:, :], AF.Ln, bias=cst[:, 0:1], scale=1.0 / 128)
            S.op("act", "activation", rs2[:, :], rs2[:, :], AF.Exp, scale=-0.5)
        if sX is not None:
            S.op("dve", "tensor_tensor", kt[:, pX, :], kk[:, :], ri[:, :], ALU.mult)
            ktv = kt.t[:, pX, :].rearrange("p (c j) -> p c j", j=64)
            khv = kh.t[:, pX, :].rearrange("p (c j) -> p c j", j=64)
            S.op("dve", "tensor_tensor", V(khv, kh.res((slice(None), pX))), V(ktv, kt.res((slice(None), pX))),
                 V(cpv[:, :, 63:64].to_broadcast([128, 8, 64]), cp.allres()), ALU.mult)
        if sZ is not None:
            S.op("dve", "tensor_tensor", t1[:, :], ps[6][:, :], rs2[:, :], ALU.mult)
            S.op("dve", "scalar_tensor_tensor", mo[:, hZ, :], t1[:, :], vcol(224), gsb[:, pZ3, :], ALU.mult, ALU.mult)

    def hgrn_layer(j):
        rms_apply(0)
        hgrn_iter(j, 0, None, None)
        wv, wres = wget(("wi", j), 1024)
        for blk in range(4):
            for half in range(2):
                n = blk * 2 + half
                pb = ps[3 + n % 2]
                for kc in range(KC):
                    S.op("pe", "matmul", pb[:, :], xn[:, kc, blk * 128:(blk + 1) * 128],
                         V(wv[:, kc, half * 512:(half + 1) * 512], wres), start=(kc == 0), stop=(kc == KC - 1))
                S.op("act", "activation", vtm[:, blk, half * 512:(half + 1) * 512], pb[:, :], AF.Copy)
        wdone(("wi", j))
        for step in range(1, NH + 2):
            hgrn_iter(j, step if step < NH else None, step - 1 if 0 <= step - 1 < NH else None,
                      step - 2 if 0 <= step - 2 < NH else None)
        proj_residual(("wout", j), mo)

    def ffn(j, l):
        rms_apply(1 if l == 0 else 4)

        def stage1_pe(cs):
            ws = [wget(("wup", j, l, c), 256) for c in cs]
            if len(cs) == 1:
                order = [(0, half, kc) for half in range(2) for kc in range(KC)]
            else:
                order = [(i, half, kc) for kc in range(KC) for i in range(len(cs)) for half in range(2)]
            for i, half, kc in order:
                c = cs[i]
                wv, wres = ws[i]
                pb = ps[c % 2] if half == 0 else ps[2 + c % 5]
                S.op("pe", "matmul", pb[:, :], V(wv[:, kc, half * 128:(half + 1) * 128], wres), xn[:, kc, :],
                     start=(kc == 0), stop=(kc == KC - 1))
            for c in cs:
                wdone(("wup", j, l, c))

        def stage1(c, pe=True):
            if pe:
                stage1_pe([c])
            b = c % 2
            pg, pv = ps[b], ps[2 + c % 5]
            hl = l * NFC + c
            S.op("dve", "tensor_copy", G[:, b, 0:2], halo[:, hl, :])
            S.op("act", "activation", G[:, b, 2:T + 2], pg[:, :], AF.Copy)
            S.op("act", "activation", acc[:, b, :], pg[:, :], AF.Identity, bias=cb(l, c), scale=cw(l, 2, c))
            S.op("dve", "tensor_copy", halo[:, hl, :], G[:, b, T:T + 2])

        def stage2(c):
            b = c % 2
            pv = ps[2 + c % 5]
            S.op("dve", "scalar_tensor_tensor", acc[:, b, :], G[:, b, 1:T + 1], cw(l, 1, c), acc[:, b, :], ALU.mult, ALU.add)
            S.op("dve", "scalar_tensor_tensor", acc[:, b, :], G[:, b, 0:T], cw(l, 0, c), acc[:, b, :], ALU.mult, ALU.add)
            S.op("act", "activation", sl[:, b, :], acc[:, b, :], AF.Silu)
            S.op("dve", "tensor_tensor", hbuf[:, c, :], sl[:, b, :], pv[:, :], ALU.mult)

        stage1_pe([0, 1])
        stage1(0, pe=False)
        stage1(1, pe=False)
        stage2(0)
        for c in range(2, NFC + 1):
            if c < NFC:
                stage1(c)
            stage2(c - 1)
        for oc in range(KC):
            wv, wres = wget(("wdn", j, l, oc), 128)
            pb = ps[oc % 2]
            for c in range(NFC):
                S.op("pe", "matmul", pb[:, :], V(wv[:, c, :], wres), hbuf[:, c, :], start=(c == 0), stop=(c == NFC - 1))
            if oc >= 1:
                stats_mm(oc - 1)
            S.op("dve", "tensor_tensor", hT[:, oc, :], hT[:, oc, :], pb[:, :], ALU.add)
            stats_sq(oc)
            wdone(("wdn", j, l, oc))
        stats_mm(KC - 1)
        stats_fin()

    def attn_layer(j):
        rms_apply(2, dst=mo)
        rms_apply(3)
        wv, wres = wget(("wkv", j), 512)
        for g in range(2):
            for kc in range(KC):
                S.op("pe", "matmul", ps[g][:, :], V(wv[:, kc, g * 128:(g + 1) * 128], wres), mo[:, kc, :],
                     start=(kc == 0), stop=(kc == KC - 1))
            S.op("act", "activation", kT[:, g, 128:640], ps[g][:, :], AF.Copy)
        for blk in range(4):
            pb = ps[2 + blk % 2]
            for kc in range(KC):
                S.op("pe", "matmul", pb[:, 0:256], mo[:, kc, blk * 128:(blk + 1) * 128], V(wv[:, kc, 256:512], wres),
                     start=(kc == 0), stop=(kc == KC - 1))
            S.op("act", "activation", vd[:, 1 + blk, :], pb[:, 0:256], AF.Copy)
        wdone(("wkv", j))
        wv, wres = wget(("wq", j), 1024)
        for qc in range(KC):
            pb = ps[4 + qc % 2]
            for kc in range(KC):
                S.op("pe", "matmul", pb[:, :], V(wv[:, kc, qc * 128:(qc + 1) * 128], wres), xn[:, kc, :],
                     start=(kc == 0), stop=(kc == KC - 1))
            S.op("act", "activation", qT[:, qc, :], pb[:, :], AF.Copy)
        wdone(("wq", j))
        dtab_r = [V(tabs.t[:, 256 + r * 128:256 + (r + 1) * 128].rearrange("p (o c) -> p o c", o=1).to_broadcast([128, 4, 128]),
                    tabs.allres()) for r in range(2)]

        def s1(hh):
            g, qc, po, b = hh // 8, hh // 2, (hh % 2) * 64, hh % 2
            for r in range(2):
                bank = ps[2 * b + r]
                for qb in range(4):
                    slot = qb + r
                    S.op("pe", "matmul", bank[:, qb * 128:(qb + 1) * 128],
                         kT[po:po + 64, g, slot * 128:(slot + 1) * 128], qT[po:po + 64, qc, qb * 128:(qb + 1) * 128],
                         start=True, stop=True)
            for r in range(2):
                bank = ps[2 * b + r]
                S.op("dve", "scalar_tensor_tensor",
                     V(scb.t[:, b, r * 512:(r + 1) * 512].rearrange("p (q c) -> p q c", c=128), scb.res((slice(None), b))),
                     dtab_r[r], -8.0 * slopes[hh],
                     V(bank.t[:, :].rearrange("p (q c) -> p q c", c=128), bank.allres()), ALU.mult, ALU.add)
            S.op("act", "activation", pbuf[:, b, :], scb[:, b, :], AF.Exp, scale=0.125)

        def s2(hh):
            g, qc, po, b = hh // 8, hh // 2, (hh % 2) * 64, hh % 2
            PV, DN = ps[4 + b], ps[6 + b]
            pres = pbuf.res((slice(None), b))
            for qb in range(4):
                first = True
                for r in range(2):
                    if j == 0 and qb == 0 and r == 0:
                        continue
                    slot = qb + r
                    n = r * 4 + qb
                    S.op("pe", "matmul", PV[:, qb * 128:(qb + 1) * 128], vd[:, slot, g * 128:(g + 1) * 128],
                         pbuf[:, b, n * 128:(n + 1) * 128], start=first, stop=(r == 1))
                    first = False
            S.op("pe", "matmul", DN[:, :], onesb[:, :], pbuf[:, b, 512:1024], start=True, stop=False)
            if j == 0:
                S.op("pe", "matmul", DN[:, 128:512], onesb[:, :], pbuf[:, b, 128:512], start=False, stop=True)
            else:
                S.op("pe", "matmul", DN[:, :], onesb[:, :], pbuf[:, b, 0:512], start=False, stop=True)
            S.op("act", "activation", rec[po:po + 64, b, :], DN[po:po + 64, :], AF.Ln, bias=esink[po:po + 64, hh:hh + 1])
            S.op("act", "activation", rec[po:po + 64, b, :], rec[po:po + 64, b, :], AF.Exp, scale=-1.0)
            S.op("dve", "tensor_tensor", mo[po:po + 64, qc, :], PV[po:po + 64, :], rec[po:po + 64, b, :], ALU.mult)

        s1(0)
        for hh in range(16):
            if hh + 1 < 16:
                s1(hh + 1)
            s2(hh)
        for g in range(2):
            S.op("act", "activation", kT[:, g, 0:128], kT[:, g, 512:640], AF.Copy)
        S.op("act", "activation", vd[:, 0, :], vd[:, 4, :], AF.Copy)
        proj_residual(("wo", j), mo)

    for j in range(NT):
        for kc in range(KC):
            S.dma("sp", hT[:, kc, :], V(xT_d[kc, :, j * T:(j + 1) * T], []), ("dx", kc))
        rms_stats()
        if "hgrn" in phases:
            hgrn_layer(j)
        dump_h(j, 0)
        if "ffn0" in phases:
            ffn(j, 0)
        dump_h(j, 1)
        if "attn" in phases:
            attn_layer(j)
        dump_h(j, 2)
        if "ffn1" in phases:
            ffn(j, 1)
        dump_h(j, 3)
        for kc in range(KC):
            b = kc % 2
            S.op("dve", "scalar_tensor_tensor", ost[:, b, :], hT[:, kc, :], gain(5, kc), rs[:, :], ALU.mult, ALU.mult)
            S.dma("sp", V(yT_d[kc, :, j * T:(j + 1) * T], []), ost[:, b, :], ("dy", b), is_output=True)
    S.emit()
    return nc


def _chunked(w):
    n = w.shape[1]
    return np.ascontiguousarray(w.reshape(KC, 128, n).transpose(1, 0, 2)).reshape(128, KC * n)


def host_layout(inputs, NT=8):
    f = lambda a: np.ascontiguousarray(np.asarray(a, dtype=np.float32))
    hg_w_in = f(inputs["hg_w_in"])[0]
    out = {}
    out["w_i"] = _chunked(hg_w_in[:, 2048:3072])
    w_in = np.empty((NH, 128, KC * 384), np.float32)
    for h in range(NH):
        cols = np.concatenate([hg_w_in[:, h * 128:(h + 1) * 128],
                               hg_w_in[:, 3072 + h * 128:3072 + (h + 1) * 128],
                               hg_w_in[:, 1024 + h * 128:1024 + (h + 1) * 128]], axis=1)
        w_in[h] = _chunked(cols)
    out["w_in"] = w_in
    out["w_out"] = _chunked(f(inputs["hg_w_out"])[0])
    w_up_in = f(inputs["ffn_w_up"])
    w_up = np.empty((2, NFC, 128, KC * 256), np.float32)
    for l in range(2):
        for c in range(NFC):
            cols = np.concatenate([w_up_in[l][:, c * 128:(c + 1) * 128],
                                   w_up_in[l][:, DFF + c * 128:DFF + (c + 1) * 128]], axis=1)
            w_up[l, c] = _chunked(cols)
    out["w_up"] = w_up
    w_dn_in = f(inputs["ffn_w_down"])
    w_dn = np.empty((2, KC, 128, NFC * 128), np.float32)
    for l in range(2):
        wd = w_dn_in[l].reshape(NFC, 128, KC, 128)
        w_dn[l] = wd.transpose(2, 1, 0, 3).reshape(KC, 128, NFC * 128)
    out["w_down"] = w_dn
    w_kv = f(inputs["w_kv"])
    k0, k1, v0, v1 = w_kv[:, 0:64], w_kv[:, 64:128], w_kv[:, 128:192], w_kv[:, 192:256]
    out["w_kv"] = _chunked(np.concatenate([k0, k0, k1, k1, v0, v0, v1, v1], axis=1))
    out["w_q"] = _chunked(f(inputs["attn_w_q"])[0])
    out["w_o"] = _chunked(f(inputs["attn_w_o"])[0])
    vecs = np.zeros((128, NV), np.float32)
    gl = [inputs["hg_norm"][0], inputs["ffn_norm"][0], inputs["kv_norm"], inputs["attn_norm"][0],
          inputs["ffn_norm"][1], inputs["final_norm"]]
    for gi, gvec in enumerate(gl):
        vecs[:, gi * 8:(gi + 1) * 8] = f(gvec).reshape(KC, 128).T
    cwv = f(inputs["ffn_conv_w"])
    cbv = f(inputs["ffn_conv_b"])
    for l in range(2):
        for jj in range(3):
            vecs[:, 48 + (l * 3 + jj) * NFC:48 + (l * 3 + jj + 1) * NFC] = cwv[l, jj].reshape(NFC, 128).T
        vecs[:, 180 + l * NFC:180 + (l + 1) * NFC] = cbv[l].reshape(NFC, 128).T
    vecs[:, 224] = f(inputs["hg_out_norm"])[0]
    lbl = f(inputs["hg_lb_logits"])
    for r in range(2):
        vecs[:, 225 + r * 8:225 + (r + 1) * 8] = lbl[r].reshape(NH, 128).T
    vecs[:, 241:257] = f(inputs["attn_sinks"])[0][None, :]
    out["vecs"] = vecs
    tabs = np.zeros((128, NTB), np.float32)
    tabs[:, 0:128] = np.eye(128, dtype=np.float32)
    s = np.arange(128)[:, None]
    t = np.arange(128)[None, :]
    tabs[:, 128:256] = ((s // 64 == t // 64) & (s <= t)).astype(np.float32)
    BIG = 1.0e6
    tabs[:, 256:384] = np.where(t < s, 128.0 + t - s, BIG)
    tabs[:, 384:512] = np.where(t >= s, (t - s).astype(np.float64), BIG)
    out["tabs"] = tabs
    return out


_NC_CACHE = {}


def kernel(**inputs):
    x = np.asarray(inputs["x"], dtype=np.float32)
    nb, s_, d_ = x.shape
    NT = s_ // T
    shared = host_layout(inputs, NT)
    if NT not in _NC_CACHE:
        _NC_CACHE[NT] = build(NT)
    nc = _NC_CACHE[NT]
    in_maps = []
    for b in range(nb):
        m = dict(shared)
        m["xT"] = np.ascontiguousarray(x[b].T).reshape(KC, 128, s_)
        in_maps.append(m)
    res = run_bass_kernel_spmd(nc, in_maps, core_ids=list(range(nb)))
    out = np.empty((nb, s_, d_), np.float32)
    for b in range(nb):
        out[b] = res.results[b]["yT"].reshape(D, s_).T
    return out
```

```python
import contextlib
import math
import numpy as np
import concourse.bass as bass
import concourse.mybir as mybir
from concourse.bass_utils import run_bass_kernel_spmd

F32 = mybir.dt.float32
BF16 = mybir.dt.bfloat16
AF = mybir.ActivationFunctionType
ALU = mybir.AluOpType

D = 1024
T = 512
KC = 8
NH = 8
DFF = 2816
NFC = 22
SEQ = 4096
EPS = 1e-6
NV = 257
NTB = 512
RING = 18432
PAGE = 1024
NWSEM = 8
LOOK = 3
USE_SCRATCH = True
HOIST = True
VCLOCK = True
HG_LEVEL = 9
DB2 = 3
CHAIN = 1
DELTA = 1
ENGS = ("pe", "dve", "act", "pool", "sp")


class V:
    __slots__ = ("ap", "res")

    def __init__(self, ap, res):
        self.ap = ap
        self.res = tuple(res)


class Sched:
    def __init__(self, nc):
        self.nc = nc
        self.ops = {e: [] for e in ENGS}
        self.last_w = {}
        self.readers = {}
        self.known = {e: {} for e in ENGS}
        self.dma_cnt = {}
        self.nseq = {e: 0 for e in ENGS}
        self.out_tokens = []
        self.gidx = 0
        self.tok_idx = {}
        self.clock = {}

    def _deps(self, eng, reads, writes):
        deps = {}

        def add(tok, same_ok):
            key, val, src = tok
            if src == eng and not same_ok:
                return
            if deps.get(key, 0) < val:
                deps[key] = val

        for r in reads:
            t = self.last_w.get(r)
            if t is not None:
                add(t, eng != "pe")
        for w in writes:
            t = self.last_w.get(w)
            if t is not None:
                add(t, eng != "pe")
            for key, (val, src) in self.readers.get(w, {}).items():
                add((key, val, src), eng != "pe")
        waits = []
        kn = self.known[eng]
        for key, val in sorted(deps.items(), key=lambda kv: -self.tok_idx.get(kv, 0)):
            if kn.get(key, 0) < val:
                waits.append((key, val))
                if VCLOCK:
                    for k2, v2 in self.clock.get((key, val), {}).items():
                        if kn.get(k2, 0) < v2:
                            kn[k2] = v2
                kn[key] = max(kn.get(key, 0), val)
        return waits

    def _commit(self, tok, reads, writes):
        key, val, src = tok
        for r in reads:
            d = self.readers.setdefault(r, {})
            if d.get(key, (0, None))[0] < val:
                d[key] = (val, src)
        for w in writes:
            self.last_w[w] = tok
            self.readers[w] = {}

    @staticmethod
    def _split(args, kw):
        reads, writes = [], []
        a2 = []
        for i, a in enumerate(args):
            if isinstance(a, V):
                (writes if i == 0 else reads).extend(a.res)
                a2.append(a.ap)
            else:
                a2.append(a)
        k2 = {}
        for k, a in kw.items():
            if isinstance(a, V):
                (writes if k in ("out", "accum_out") else reads).extend(a.res)
                k2[k] = a.ap
            else:
                k2[k] = a
        return a2, k2, reads, writes

    def op(self, eng, method, *args, xr=(), xw=(), **kw):
        a2, k2, reads, writes = self._split(args, kw)
        reads += list(xr)
        writes += list(xw)
        waits = self._deps(eng, reads, writes)
        self.nseq[eng] += 1
        tok = (eng, self.nseq[eng], eng)
        self._commit(tok, reads, writes)
        self.gidx += 1
        self.tok_idx[(eng, self.nseq[eng])] = self.gidx
        ck = dict(self.known[eng])
        ck[eng] = self.nseq[eng]
        self.clock[(eng, self.nseq[eng])] = ck
        self.ops[eng].append((waits, method, a2, k2, (eng, 1), self.gidx))
        return tok

    def dma(self, eng, out, in_, semkey, is_output=False, xw=(), **kw):
        a2, k2, reads, writes = self._split((out, in_), kw)
        writes += list(xw) + [("sem", semkey)]
        waits = self._deps(eng, reads, writes)
        self.dma_cnt[semkey] = self.dma_cnt.get(semkey, 0) + 16
        tok = (semkey, self.dma_cnt[semkey], "dma")
        self._commit(tok, reads, writes)
        self.gidx += 1
        self.tok_idx[(semkey, self.dma_cnt[semkey])] = self.gidx
        ck = dict(self.known[eng])
        ck[semkey] = self.dma_cnt[semkey]
        self.clock[(semkey, self.dma_cnt[semkey])] = ck
        self.ops[eng].append((waits, "dma_start", a2, k2, (semkey, 16), self.gidx))
        if is_output:
            self.out_tokens.append(tok)
        return tok

    def emit(self):
        nc = self.nc
        keys = set(ENGS) | set(self.dma_cnt.keys())
        keys = sorted(keys, key=str)
        with contextlib.ExitStack() as es:
            sems = {}
            for i, k in enumerate(keys):
                sems[k] = es.enter_context(nc.semaphore("s%d" % i))
            block = es.enter_context(nc.Block())
            fin = {}
            for key, val, _ in self.out_tokens:
                fin[key] = max(fin.get(key, 0), val)

            def plan(name):
                ops = self.ops[name]
                att = [None] * len(ops)
                alone = [[] for _ in ops]
                for i, (waits, method, a, k, inc, gi) in enumerate(ops):
                    if not waits:
                        continue
                    if method == "dma_start" or not HOIST:
                        alone[i] = list(waits)
                        continue
                    ws = sorted(waits, key=lambda w: -self.tok_idx.get((w[0], w[1]), 0))
                    att[i] = ws[0]
                    for w in ws[1:]:
                        ti = self.tok_idx.get((w[0], w[1]), 0)
                        placed = False
                        for i2 in range(i - 1, max(i - 24, -1), -1):
                            if ops[i2][5] <= ti:
                                break
                            if att[i2] is None and ops[i2][1] != "dma_start" and not alone[i2]:
                                att[i2] = w
                                placed = True
                                break
                        if not placed:
                            alone[i].append(w)
                return att, alone

            def run(engine, name):
                att, alone = plan(name)
                for i, (waits, method, a, k, (skey, amt), gi) in enumerate(self.ops[name]):
                    for wk, wv in alone[i]:
                        engine.wait_ge(sems[wk], wv)
                    inst = getattr(engine, method)(*a, **k)
                    if att[i] is not None:
                        inst._wait_ge(sems[att[i][0]], att[i][1])
                    inst.then_inc(sems[skey], amt)
                if name == "sp":
                    for key, val in fin.items():
                        engine.wait_ge(sems[key], val)

            @block.tensor
            def _(e):
                run(e, "pe")

            @block.vector
            def _(e):
                run(e, "dve")

            @block.scalar
            def _(e):
                run(e, "act")

            @block.gpsimd
            def _(e):
                run(e, "pool")

            @block.sync
            def _(e):
                run(e, "sp")


class Buf:
    def __init__(self, nc, name, shape, dtype, axis=1, g=None, psum=False):
        if psum:
            self.t = nc.alloc_psum_tensor("t_" + name, list(shape), dtype)
        else:
            self.t = nc.alloc_sbuf_tensor("t_" + name, list(shape), dtype)
        self.name = name
        self.shape = tuple(shape)
        self.axis = axis
        self.g = g if g is not None else shape[axis]

    def res(self, idx):
        if not isinstance(idx, tuple):
            idx = (idx,)
        n = self.shape[self.axis]
        if self.axis < len(idx):
            s = idx[self.axis]
            if isinstance(s, int):
                lo, hi = s, s + 1
            else:
                lo = 0 if s.start is None else s.start
                hi = n if s.stop is None else s.stop
        else:
            lo, hi = 0, n
        return [(self.name, i) for i in range(lo // self.g, (hi - 1) // self.g + 1)]

    def __getitem__(self, idx):
        return V(self.t[idx], self.res(idx))

    def allres(self):
        return self.res(())


def alibi_slopes():
    return [float(np.float32(2.0 ** (-8.0 * (h + 1) / 16))) for h in range(16)]


def build(NT=8, dump=False, phases=("hgrn", "ffn0", "attn", "ffn1")):
    S_ = NT * T
    nc = bass.Bass("TRN2", target_bir_lowering=False)
    S = Sched(nc)

    def din(name, shape):
        return nc.dram_tensor(name, list(shape), F32, kind="ExternalInput").ap()

    xT_d = din("xT", [KC, 128, S_])
    wi_d = din("w_i", [128, 8 * 1024])
    win_d = din("w_in", [NH, 128, 8 * 384])
    wout_d = din("w_out", [128, 8 * 1024])
    wup_d = din("w_up", [2, NFC, 128, 8 * 256])
    wdn_d = din("w_down", [2, KC, 128, NFC * 128])
    wkv_d = din("w_kv", [128, 8 * 512])
    wq_d = din("w_q", [128, 8 * 1024])
    wo_d = din("w_o", [128, 8 * 1024])
    vecs_d = din("vecs", [128, NV])
    tabs_d = din("tabs", [128, NTB])
    yT_d = nc.dram_tensor("yT", [KC, 128, S_], F32, kind="ExternalOutput").ap()
    dbg_d = None
    if dump:
        dbg_d = nc.dram_tensor("dbg", [4, KC, 128, S_], F32, kind="ExternalOutput").ap()

    B = lambda *a, **k: Buf(nc, *a, **k)
    hT = B("hT", [128, KC, T], F32, axis=1, g=1)
    xn = B("xn", [128, KC, T], BF16, axis=1, g=1)
    xsq = B("xsq", [128, 2, T], BF16, axis=1, g=1)
    rs = B("rs", [128, T], F32)
    rv = B("rv", [128, T], F32)
    cst = B("cst", [128, 8], F32)
    vtm = B("vtm", [128, 4, 1024], BF16, axis=1, g=1)
    mo = B("mo", [128, KC, T], BF16, axis=1, g=1)
    qs = B("qs", [128, T], F32)
    gsb = B("gsb", [128, 3, T], F32, axis=1, g=1)
    th = B("th", [128, T], F32)
    fg = B("fg", [128, T], F32)
    kk = B("kk", [128, T], F32)
    cp = B("cp", [128, T], F32)
    ri = B("ri", [128, T], F32)
    d1 = B("d1", [128, T], F32)
    qt = B("qt", [128, 3, T], BF16, axis=1, g=1)
    kt = B("kt", [128, 3, T], BF16, axis=1, g=1)
    kh = B("kh", [128, 3, T], BF16, axis=1, g=1)
    khT = B("khT", [128, 2, T], BF16, axis=1, g=1)
    Am = B("Am", [128, 2, T], BF16, axis=1, g=1)
    eb = B("eb", [128, 3, 8], F32, axis=1, g=1)
    osq = B("osq", [128, T], BF16)
    t1 = B("t1", [128, T], F32)
    S32 = B("S32", [128, NH, 128], F32, axis=1, g=1)
    SbP = B("SbP", [128, 2 * NH, 128], BF16, axis=1, g=1)
    Rr = B("Rr", [128, 16, 128], BF16, axis=1, g=1)
    hbuf = B("hbuf", [128, NFC, T], BF16, axis=1, g=1)
    G = B("G", [128, 2, T + 2], F32, axis=1, g=1)
    acc = B("acc", [128, 2, T], F32, axis=1, g=1)
    sl = B("sl", [128, 2, T], F32, axis=1, g=1)
    halo = B("halo", [128, 2 * NFC, 2], F32, axis=1, g=1)
    qT = B("qT", [128, KC, T], BF16, axis=1, g=1)
    kT = B("kT", [128, 2, 5 * 128], BF16, axis=2, g=128)
    vd = B("vd", [128, 5, 256], BF16, axis=1, g=1)
    scb = B("scb", [128, 2, 1024], F32, axis=1, g=1)
    pbuf = B("pbuf", [128, 2, 1024], BF16, axis=1, g=1)
    rec = B("rec", [128, 2, T], F32, axis=1, g=1)
    ost = B("ost", [128, 2, T], F32, axis=1, g=1)
    vecs = B("vecs", [128, NV], F32)
    tabs = B("tabs", [128, NTB], F32)
    identb = B("identb", [128, 128], BF16)
    onesb = B("onesb", [128, 128], BF16)
    lbt = B("lbt", [128, 8], F32)
    Ac = B("Ac", [128, 8], F32)
    Bc = B("Bc", [128, 8], F32)
    nBc = B("nBc", [128, 8], F32)
    esink = B("esink", [128, 16], F32)
    WR = B("WR", [128, RING], BF16, axis=1, g=PAGE)
    ps = [B("ps%d" % i, [128, 512], F32, axis=1, g=512, psum=True) for i in range(8)]

    slopes = alibi_slopes()

    wsched = []

    scr = {}

    def sdram(name, shape):
        return nc.dram_tensor(name, list(shape), BF16).ap()

    wi_s = sdram("s_w_i", [128, 8 * 1024])
    win_s = sdram("s_w_in", [NH, 128, 8 * 384])
    wout_s = sdram("s_w_out", [128, 8 * 1024])
    wup_s = sdram("s_w_up", [2, NFC, 128, 8 * 256])
    wdn_s = sdram("s_w_down", [2, KC, 128, NFC * 128])
    wkv_s = sdram("s_w_kv", [128, 8 * 512])
    wq_s = sdram("s_w_q", [128, 8 * 1024])
    wo_s = sdram("s_w_o", [128, 8 * 1024])
    scr_of = {id(wi_d): wi_s, id(wout_d): wout_s, id(wkv_d): wkv_s, id(wq_d): wq_s, id(wo_d): wo_s}

    def wadd(key, ap, n, sap=None):
        wsched.append((key, ap, n, sap))

    for j in range(NT):
        if "hgrn" in phases:
            wadd(("win", j, 0), win_d[0], 3072, win_s[0])
            wadd(("wi", j), wi_d, 8192, wi_s)
            for h in range(1, NH):
                wadd(("win", j, h), win_d[h], 3072, win_s[h])
            wadd(("wout", j), wout_d, 8192, wout_s)
        for l in range(2):
            if l == 1 and "attn" in phases:
                wadd(("wkv", j), wkv_d, 4096, wkv_s)
                wadd(("wq", j), wq_d, 8192, wq_s)
                wadd(("wo", j), wo_d, 8192, wo_s)
            if ("ffn%d" % l) in phases:
                for c in range(NFC):
                    wadd(("wup", j, l, c), wup_d[l, c], 2048, wup_s[l, c])
                for oc in range(KC):
                    wadd(("wdn", j, l, oc), wdn_d[l, oc], 2816, wdn_s[l, oc])
    widx = {k: i for i, (k, _, _, _) in enumerate(wsched)}
    wstate = {"next": 0, "head": 0, "live": [], "views": {}, "cnt": 0}

    def _try_alloc(n):
        live = wstate["live"]
        head = wstate["head"]
        if not live:
            wstate["head"] = n
            return 0
        tail = live[0][1]
        if head >= tail:
            if head + n <= RING:
                wstate["head"] = head + n
                return head
            if n < tail:
                wstate["head"] = n
                return 0
            return None
        if head + n < tail:
            wstate["head"] = head + n
            return head
        return None

    def _emit_load(i):
        key, ap, n, sap = wsched[i]
        off = _try_alloc(n)
        if off is None:
            return False
        wstate["live"].append((key, off, off + n))
        semkey = ("dw", wstate["cnt"] % NWSEM)
        wstate["cnt"] += 1
        jt = key[1]
        sres = [("scr",) + (key[0],) + tuple(key[2:])]
        if jt == 0 or not USE_SCRATCH:
            S.dma("pool", WR[:, off:off + n], V(ap, []), semkey)
            if USE_SCRATCH and NT > 1:
                S.dma("sp", V(sap, sres), WR[:, off:off + n], ("db", wstate["cnt"] % 4))
        else:
            S.dma("pool", WR[:, off:off + n], V(sap, sres), semkey)
        wstate["views"][key] = (off, n)
        return True

    def wget(key, inner):
        i = widx[key]
        while wstate["next"] <= min(i + LOOK, len(wsched) - 1):
            if not _emit_load(wstate["next"]):
                break
            wstate["next"] += 1
        assert key in wstate["views"], ("ring too small for", key)
        off, n = wstate["views"][key]
        res = WR.res((slice(None), slice(off, off + n)))
        view = WR.t[:, off:off + n].rearrange("p (k c) -> p k c", c=inner)
        return view, res

    def wdone(key):
        k0 = wstate["live"].pop(0)
        assert k0[0] == key, (k0, key)
        del wstate["views"][key]

    def vcol(c):
        return vecs[:, c:c + 1]

    def gain(gi, kc):
        return vcol(gi * 8 + kc)

    def cw(l, jj, c):
        return vcol(48 + (l * 3 + jj) * NFC + c)

    def cb(l, c):
        return vcol(180 + l * NFC + c)

    def stats_sq(kc):
        S.op("act", "activation", xsq[:, kc % 2, :], hT[:, kc, :], AF.Square)

    def stats_mm(kc):
        S.op("pe", "matmul", ps[7][:, :], onesb[:, :], xsq[:, kc % 2, :], start=(kc == 0), stop=(kc == KC - 1))

    def stats_fin():
        S.op("act", "activation", rv[:, :], ps[7][:, :], AF.Ln, bias=cst[:, 0:1], scale=1.0 / D)
        S.op("act", "activation", rs[:, :], rv[:, :], AF.Exp, scale=-0.5)

    def rms_stats():
        for kc in range(KC):
            stats_sq(kc)
            stats_mm(kc)
        stats_fin()

    def rms_apply(gi, dst=None):
        dst = xn if dst is None else dst
        for kc in range(KC):
            S.op("dve", "scalar_tensor_tensor", dst[:, kc, :], hT[:, kc, :], gain(gi, kc), rs[:, :], ALU.mult, ALU.mult)

    def proj_residual(key, src):
        wv, wres = wget(key, 1024)
        for oc in range(KC):
            pb = ps[oc % 4]
            for kc in range(KC):
                S.op("pe", "matmul", pb[:, :], V(wv[:, kc, oc * 128:(oc + 1) * 128], wres), src[:, kc, :],
                     start=(kc == 0), stop=(kc == KC - 1))
            if oc >= 1:
                stats_mm(oc - 1)
            S.op("dve", "tensor_tensor", hT[:, oc, :], hT[:, oc, :], pb[:, :], ALU.add)
            stats_sq(oc)
        stats_mm(KC - 1)
        stats_fin()
        wdone(key)

    def dump_h(j, stage):
        if dbg_d is None:
            return
        for kc in range(KC):
            S.dma("sp", V(dbg_d[stage, kc, :, j * T:(j + 1) * T], []), hT[:, kc, :], ("dd", kc), is_output=True)

    S.dma("sp", vecs[:, :], V(vecs_d, []), ("ds", 0))
    S.dma("sp", tabs[:, :], V(tabs_d, []), ("ds", 1))
    S.op("pool", "memset", onesb[:, :], 1.0)
    S.op("pool", "memset", cst[:, :], EPS)
    S.op("pool", "memset", halo[:, :, :], 0.0)
    S.op("pool", "memset", S32[:, :, :], 0.0)
    S.op("pool", "memset", SbP[:, :, :], 0.0)
    S.op("pool", "memset", kT[:, :, :], 0.0)
    S.op("pool", "memset", vd[:, :, :], 0.0)
    S.op("pool", "memset", d1[:, :], 0.0)
    S.op("dve", "tensor_copy", identb[:, :], tabs[:, 0:128])
    S.op("dve", "tensor_tensor", lbt[:, :], vecs[:, 225:233], vecs[:, 233:241], ALU.subtract)
    S.op("act", "activation", lbt[:, :], lbt[:, :], AF.Tanh, scale=0.5)
    S.op("dve", "tensor_scalar", Ac[:, :], lbt[:, :], 0.25, 0.75, ALU.mult, ALU.add)
    S.op("dve", "tensor_scalar", Bc[:, :], lbt[:, :], -0.25, 0.25, ALU.mult, ALU.add)
    S.op("dve", "tensor_scalar", nBc[:, :], lbt[:, :], 0.25, -0.25, ALU.mult, ALU.add)
    S.op("act", "activation", esink[:, :], vecs[:, 241:257], AF.Exp)

    amask = V(tabs.t[:, 128:256].rearrange("p (o c) -> p o c", o=1).to_broadcast([128, 4, 128]), tabs.allres())
    dtab = V(tabs.t[:, 256:512].rearrange("p (o c) -> p o c", o=1).to_broadcast([128, 2, 256]), tabs.allres())

    rs2 = B("rs2", [128, T], F32)

    def hgrn_iter(j, sX, sY, sZ):
        if sX is not None:
            hX, pX = sX, sX % 3
            wv, wres = wget(("win", j, hX), 384)
        if sY is not None:
            hY, pY3, pY2 = sY, sY % 3, sY % 2
            wrY = ((j + 1) % 2) * NH + hY
            pt3 = ps[3].t[:, :].bitcast(BF16)
            khTv = khT.t[:, pY2, :].rearrange("p (b k) -> p b k", k=128)
            AmvY = Am.t[:, pY2, :].rearrange("p (b k) -> p b k", k=128)
        if sZ is not None:
            hZ, pZ3, pZ2 = sZ, sZ % 3, sZ % 2
            rdZ = (j % 2) * NH + hZ
            AmvZ = Am.t[:, pZ2, :].rearrange("p (b k) -> p b k", k=128)
        def projX(idx):
            for kc in range(KC):
                S.op("pe", "matmul", ps[idx][:, :], V(wv[:, kc, idx * 128:(idx + 1) * 128], wres), xn[:, kc, :],
                     start=(kc == 0), stop=(kc == KC - 1))
        if sX is not None:
            projX(2)
            S.op("act", "activation", th[:, :], ps[2][:, :], AF.Tanh, scale=0.5)
            S.op("pool", "tensor_scalar", fg[:, :], th[:, :], Bc[:, hX:hX + 1], Ac[:, hX:hX + 1], ALU.mult, ALU.add)
            S.op("pool", "tensor_scalar", kk[:, :], th[:, :], nBc[:, hX:hX + 1], Bc[:, hX:hX + 1], ALU.mult, ALU.add)
            fgv = fg.t[:, :].rearrange("p (c j) -> p c j", j=64)
            d1v = d1.t[:, :].rearrange("p (c j) -> p c j", j=64)
            S.op("dve", "tensor_copy", V(d1v[:, :, 0:1], d1.allres()), V(fgv[:, :, 0:1], fg.allres()))
            S.op("dve", "memset", V(fgv[:, :, 0:1], fg.allres()), 0.0)
            S.op("dve", "tensor_tensor_scan", cp[:, :], fg[:, :], d1[:, :], 0.0, ALU.mult, ALU.add)
        if sY is not None:
            for pr in range(4):
                S.op("pe", "transpose", V(pt3[:, pr * 128:(pr + 1) * 128], ps[3].allres()), kh[:, pY3, pr * 128:(pr + 1) * 128], identb[:, :])
        if sZ is not None:
            for pr in range(4):
                S.op("pe", "matmul", ps[6][:, pr * 128:(pr + 1) * 128], vtm[:, pr, hZ * 128:(hZ + 1) * 128],
                     V(AmvZ[:, pr, :], Am.res((slice(None), pZ2))), start=True, stop=False)
                for half in range(2):
                    c = 2 * pr + half
                    sprev = SbP[:, rdZ, :] if c == 0 else Rr[:, pZ2 * 8 + c - 1, :]
                    S.op("pe", "matmul", ps[6][:, c * 64:(c + 1) * 64], sprev, qt[:, pZ3, c * 64:(c + 1) * 64],
                         start=False, stop=(half == 1))
        if sY is not None:
            S.op("act", "activation", khT[:, pY2, :], V(pt3[:, 0:512], ps[3].allres()), AF.Copy)
        if sZ is not None:
            S.op("act", "activation", osq[:, :], ps[6][:, :], AF.Square)
        if sY is not None:
            for c in range(8):
                blk, half = c // 2, c % 2
                dbank = ps[4] if c % 2 == 0 else ps[3]
                S.op("pe", "matmul", dbank[:, (c // 2) * 128:(c // 2 + 1) * 128],
                     V(khTv[half * 64:(half + 1) * 64, blk, :], khT.res((slice(None), pY2))),
                     vtm[half * 64:(half + 1) * 64, blk, hY * 128:(hY + 1) * 128], start=True, stop=True)
            for pr in range(4):
                S.op("pe", "matmul", ps[5][:, pr * 128:(pr + 1) * 128], kt[:, pY3, pr * 128:(pr + 1) * 128],
                     qt[:, pY3, pr * 128:(pr + 1) * 128], start=True, stop=True)
        if sZ is not None:
            S.op("pe", "matmul", ps[7][:, :], onesb[:, :], osq[:, :], start=True, stop=True)
        if sX is not None:
            projX(1)
            projX(0)
            wdone(("win", j, hX))
            S.op("act", "activation", ri[:, :], cp[:, :], AF.Ln)
            S.op("act", "activation", ri[:, :], ri[:, :], AF.Exp, scale=-1.0)
        if sY is not None:
            S.op("dve", "tensor_tensor", V(AmvY, Am.res((slice(None), pY2))),
                 V(ps[5].t[:, :].rearrange("p (b k) -> p b k", k=128), ps[5].allres()), amask, ALU.mult)
            for c in range(8):
                dbank = ps[4] if c % 2 == 0 else ps[3]
                S.op("dve", "scalar_tensor_tensor", S32[:, hY, :], S32[:, hY, :], eb[:, pY3, c:c + 1],
                     dbank[:, (c // 2) * 128:(c // 2 + 1) * 128], ALU.mult, ALU.add)
                dst = Rr[:, pY2 * 8 + c, :] if c < 7 else SbP[:, wrY, :]
                S.op("dve", "tensor_copy", dst, S32[:, hY, :])
        if sZ is not None:
            S.op("act", "activation", rs2[:, :], ps[7][:, :], AF.Ln, bias=cst[:, 0:1], scale=1.0 / 128)
            S.op("act", "activation", rs2[:, :], rs2[:, :], AF.Exp, scale=-0.5)
        if sX is not None:
            S.op("act", "activation", gsb[:, pX, :], ps[1][:, :], AF.Silu)
            S.op("act", "activation", qs[:, :], ps[0][:, :], AF.Silu)
            S.op("dve", "scalar_tensor_tensor", qt[:, pX, :], qs[:, :], 128 ** -0.5, cp[:, :], ALU.mult, ALU.mult)
            cpv = cp.t[:, :].rearrange("p (c j) -> p c j", j=64)
            S.op("dve", "tensor_copy", V(eb.t[:, pX, :].rearrange("p (c o) -> p c o", o=1), eb.res((slice(None), pX))),
                 V(cpv[:, :, 63:64], cp.allres()))
        if sX is not None:
            S.op("dve", "tensor_tensor", kt[:, pX, :], kk[:, :], ri[:, :], ALU.mult)
            ktv = kt.t[:, pX, :].rearrange("p (c j) -> p c j", j=64)
            khv = kh.t[:, pX, :].rearrange("p (c j) -> p c j", j=64)
            S.op("dve", "tensor_tensor", V(khv, kh.res((slice(None), pX))), V(ktv, kt.res((slice(None), pX))),
                 V(cpv[:, :, 63:64].to_broadcast([128, 8, 64]), cp.allres()), ALU.mult)
        if sZ is not None:
            S.op("dve", "tensor_tensor", t1[:, :], ps[6][:, :], rs2[:, :], ALU.mult)
            S.op("dve", "scalar_tensor_tensor", mo[:, hZ, :], t1[:, :], vcol(224), gsb[:, pZ3, :], ALU.mult, ALU.mult)

    def hgrn_layer(j):
        rms_apply(0)
        hgrn_iter(j, 0, None, None)
        wv, wres = wget(("wi", j), 1024)
        for blk in range(4):
            for half in range(2):
                n = blk * 2 + half
                pb = ps[3 + n % 2]
                for kc in range(KC):
                    S.op("pe", "matmul", pb[:, :], xn[:, kc, blk * 128:(blk + 1) * 128],
                         V(wv[:, kc, half * 512:(half + 1) * 512], wres), start=(kc == 0), stop=(kc == KC - 1))
                S.op("act", "activation", vtm[:, blk, half * 512:(half + 1) * 512], pb[:, :], AF.Copy)
        wdone(("wi", j))
        for step in range(1, NH + 2):
            hgrn_iter(j, step if step < NH else None, step - 1 if 0 <= step - 1 < NH else None,
                      step - 2 if 0 <= step - 2 < NH else None)
        proj_residual(("wout", j), mo)

    def ffn(j, l):
        rms_apply(1 if l == 0 else 4)

        def stage1_pe(cs):
            ws = [wget(("wup", j, l, c), 256) for c in cs]
            if len(cs) == 1:
                order = [(0, half, kc) for half in range(2) for kc in range(KC)]
            else:
                order = [(i, half, kc) for kc in range(KC) for i in range(len(cs)) for half in range(2)]
            for i, half, kc in order:
                c = cs[i]
                wv, wres = ws[i]
                pb = ps[c % 2] if half == 0 else ps[2 + c % 5]
                S.op("pe", "matmul", pb[:, :], V(wv[:, kc, half * 128:(half + 1) * 128], wres), xn[:, kc, :],
                     start=(kc == 0), stop=(kc == KC - 1))
            for c in cs:
                wdone(("wup", j, l, c))

        def stage1(c, pe=True):
            if pe:
                stage1_pe([c])
            b = c % 2
            pg, pv = ps[b], ps[2 + c % 5]
            hl = l * NFC + c
            S.op("dve", "tensor_copy", G[:, b, 0:2], halo[:, hl, :])
            S.op("act", "activation", G[:, b, 2:T + 2], pg[:, :], AF.Copy)
            S.op("act", "activation", acc[:, b, :], pg[:, :], AF.Identity, bias=cb(l, c), scale=cw(l, 2, c))
            S.op("dve", "tensor_copy", halo[:, hl, :], G[:, b, T:T + 2])

        def stage2(c):
            b = c % 2
            pv = ps[2 + c % 5]
            S.op("dve", "scalar_tensor_tensor", acc[:, b, :], G[:, b, 1:T + 1], cw(l, 1, c), acc[:, b, :], ALU.mult, ALU.add)
            S.op("dve", "scalar_tensor_tensor", acc[:, b, :], G[:, b, 0:T], cw(l, 0, c), acc[:, b, :], ALU.mult, ALU.add)
            S.op("act", "activation", sl[:, b, :], acc[:, b, :], AF.Silu)
            S.op("dve", "tensor_tensor", hbuf[:, c, :], sl[:, b, :], pv[:, :], ALU.mult)

        stage1_pe([0, 1])
        stage1(0, pe=False)
        stage1(1, pe=False)
        stage2(0)
        for c in range(2, NFC + 1):
            if c < NFC:
                stage1(c)
            stage2(c - 1)
        for oc in range(KC):
            wv, wres = wget(("wdn", j, l, oc), 128)
            pb = ps[oc % 2]
            for c in range(NFC):
                S.op("pe", "matmul", pb[:, :], V(wv[:, c, :], wres), hbuf[:, c, :], start=(c == 0), stop=(c == NFC - 1))
            if oc >= 1:
                stats_mm(oc - 1)
            S.op("dve", "tensor_tensor", hT[:, oc, :], hT[:, oc, :], pb[:, :], ALU.add)
            stats_sq(oc)
            wdone(("wdn", j, l, oc))
        stats_mm(KC - 1)
        stats_fin()

    def attn_layer(j):
        rms_apply(2, dst=mo)
        rms_apply(3)
        wv, wres = wget(("wkv", j), 512)
        for g in range(2):
            for kc in range(KC):
                S.op("pe", "matmul", ps[g][:, :], V(wv[:, kc, g * 128:(g + 1) * 128], wres), mo[:, kc, :],
                     start=(kc == 0), stop=(kc == KC - 1))
            S.op("act", "activation", kT[:, g, 128:640], ps[g][:, :], AF.Copy)
        for blk in range(4):
            pb = ps[2 + blk % 2]
            for kc in range(KC):
                S.op("pe", "matmul", pb[:, 0:256], mo[:, kc, blk * 128:(blk + 1) * 128], V(wv[:, kc, 256:512], wres),
                     start=(kc == 0), stop=(kc == KC - 1))
            S.op("act", "activation", vd[:, 1 + blk, :], pb[:, 0:256], AF.Copy)
        wdone(("wkv", j))
        wv, wres = wget(("wq", j), 1024)
        for qc in range(KC):
            pb = ps[4 + qc % 2]
            for kc in range(KC):
                S.op("pe", "matmul", pb[:, :], V(wv[:, kc, qc * 128:(qc + 1) * 128], wres), xn[:, kc, :],
                     start=(kc == 0), stop=(kc == KC - 1))
            S.op("act", "activation", qT[:, qc, :], pb[:, :], AF.Copy)
        wdone(("wq", j))
        dtab_r = [V(tabs.t[:, 256 + r * 128:256 + (r + 1) * 128].rearrange("p (o c) -> p o c", o=1).to_broadcast([128, 4, 128]),
                    tabs.allres()) for r in range(2)]

        def s1(hh):
            g, qc, po, b = hh // 8, hh // 2, (hh % 2) * 64, hh % 2
            for r in range(2):
                bank = ps[2 * b + r]
                for qb in range(4):
                    slot = qb + r
                    S.op("pe", "matmul", bank[:, qb * 128:(qb + 1) * 128],
                         kT[po:po + 64, g, slot * 128:(slot + 1) * 128], qT[po:po + 64, qc, qb * 128:(qb + 1) * 128],
                         start=True, stop=True)
            for r in range(2):
                bank = ps[2 * b + r]
                S.op("dve", "scalar_tensor_tensor",
                     V(scb.t[:, b, r * 512:(r + 1) * 512].rearrange("p (q c) -> p q c", c=128), scb.res((slice(None), b))),
                     dtab_r[r], -8.0 * slopes[hh],
                     V(bank.t[:, :].rearrange("p (q c) -> p q c", c=128), bank.allres()), ALU.mult, ALU.add)
            S.op("act", "activation", pbuf[:, b, :], scb[:, b, :], AF.Exp, scale=0.125)

        def s2(hh):
            g, qc, po, b = hh // 8, hh // 2, (hh % 2) * 64, hh % 2
            PV, DN = ps[4 + b], ps[6 + b]
            pres = pbuf.res((slice(None), b))
            for qb in range(4):
                first = True
                for r in range(2):
                    if j == 0 and qb == 0 and r == 0:
                        continue
                    slot = qb + r
                    n = r * 4 + qb
                    S.op("pe", "matmul", PV[:, qb * 128:(qb + 1) * 128], vd[:, slot, g * 128:(g + 1) * 128],
                         pbuf[:, b, n * 128:(n + 1) * 128], start=first, stop=(r == 1))
                    first = False
            S.op("pe", "matmul", DN[:, :], onesb[:, :], pbuf[:, b, 512:1024], start=True, stop=False)
            if j == 0:
                S.op("pe", "matmul", DN[:, 128:512], onesb[:, :], pbuf[:, b, 128:512], start=False, stop=True)
            else:
                S.op("pe", "matmul", DN[:, :], onesb[:, :], pbuf[:, b, 0:512], start=False, stop=True)
            S.op("act", "activation", rec[po:po + 64, b, :], DN[po:po + 64, :], AF.Ln, bias=esink[po:po + 64, hh:hh + 1])
            S.op("act", "activation", rec[po:po + 64, b, :], rec[po:po + 64, b, :], AF.Exp, scale=-1.0)
            S.op("dve", "tensor_tensor", mo[po:po + 64, qc, :], PV[po:po + 64, :], rec[po:po + 64, b, :], ALU.mult)

        s1(0)
        for hh in range(16):
            if hh + 1 < 16:
                s1(hh + 1)
            s2(hh)
        for g in range(2):
            S.op("act", "activation", kT[:, g, 0:128], kT[:, g, 512:640], AF.Copy)
        S.op("act", "activation", vd[:, 0, :], vd[:, 4, :], AF.Copy)
        proj_residual(("wo", j), mo)

    for j in range(NT):
        for kc in range(KC):
            S.dma("sp", hT[:, kc, :], V(xT_d[kc, :, j * T:(j + 1) * T], []), ("dx", kc))
        rms_stats()
        if "hgrn" in phases:
            hgrn_layer(j)
        dump_h(j, 0)
        if "ffn0" in phases:
            ffn(j, 0)
        dump_h(j, 1)
        if "attn" in phases:
            attn_layer(j)
        dump_h(j, 2)
        if "ffn1" in phases:
            ffn(j, 1)
        dump_h(j, 3)
        for kc in range(KC):
            b = kc % 2
            S.op("dve", "scalar_tensor_tensor", ost[:, b, :], hT[:, kc, :], gain(5, kc), rs[:, :], ALU.mult, ALU.mult)
            S.dma("sp", V(yT_d[kc, :, j * T:(j + 1) * T], []), ost[:, b, :], ("dy", b), is_output=True)
    S.emit()
    return nc


def _chunked(w):
    n = w.shape[1]
    return np.ascontiguousarray(w.reshape(KC, 128, n).transpose(1, 0, 2)).reshape(128, KC * n)


def host_layout(inputs, NT=8):
    f = lambda a: np.ascontiguousarray(np.asarray(a, dtype=np.float32))
    hg_w_in = f(inputs["hg_w_in"])[0]
    out = {}
    out["w_i"] = _chunked(hg_w_in[:, 2048:3072])
    w_in = np.empty((NH, 128, KC * 384), np.float32)
    for h in range(NH):
        cols = np.concatenate([hg_w_in[:, h * 128:(h + 1) * 128],
                               hg_w_in[:, 3072 + h * 128:3072 + (h + 1) * 128],
                               hg_w_in[:, 1024 + h * 128:1024 + (h + 1) * 128]], axis=1)
        w_in[h] = _chunked(cols)
    out["w_in"] = w_in
    out["w_out"] = _chunked(f(inputs["hg_w_out"])[0])
    w_up_in = f(inputs["ffn_w_up"])
    w_up = np.empty((2, NFC, 128, KC * 256), np.float32)
    for l in range(2):
        for c in range(NFC):
            cols = np.concatenate([w_up_in[l][:, c * 128:(c + 1) * 128],
                                   w_up_in[l][:, DFF + c * 128:DFF + (c + 1) * 128]], axis=1)
            w_up[l, c] = _chunked(cols)
    out["w_up"] = w_up
    w_dn_in = f(inputs["ffn_w_down"])
    w_dn = np.empty((2, KC, 128, NFC * 128), np.float32)
    for l in range(2):
        wd = w_dn_in[l].reshape(NFC, 128, KC, 128)
        w_dn[l] = wd.transpose(2, 1, 0, 3).reshape(KC, 128, NFC * 128)
    out["w_down"] = w_dn
    w_kv = f(inputs["w_kv"])
    k0, k1, v0, v1 = w_kv[:, 0:64], w_kv[:, 64:128], w_kv[:, 128:192], w_kv[:, 192:256]
    out["w_kv"] = _chunked(np.concatenate([k0, k0, k1, k1, v0, v0, v1, v1], axis=1))
    out["w_q"] = _chunked(f(inputs["attn_w_q"])[0])
    out["w_o"] = _chunked(f(inputs["attn_w_o"])[0])
    vecs = np.zeros((128, NV), np.float32)
    gl = [inputs["hg_norm"][0], inputs["ffn_norm"][0], inputs["kv_norm"], inputs["attn_norm"][0],
          inputs["ffn_norm"][1], inputs["final_norm"]]
    for gi, gvec in enumerate(gl):
        vecs[:, gi * 8:(gi + 1) * 8] = f(gvec).reshape(KC, 128).T
    cwv = f(inputs["ffn_conv_w"])
    cbv = f(inputs["ffn_conv_b"])
    for l in range(2):
        for jj in range(3):
            vecs[:, 48 + (l * 3 + jj) * NFC:48 + (l * 3 + jj + 1) * NFC] = cwv[l, jj].reshape(NFC, 128).T
        vecs[:, 180 + l * NFC:180 + (l + 1) * NFC] = cbv[l].reshape(NFC, 128).T
    vecs[:, 224] = f(inputs["hg_out_norm"])[0]
    lbl = f(inputs["hg_lb_logits"])
    for r in range(2):
        vecs[:, 225 + r * 8:225 + (r + 1) * 8] = lbl[r].reshape(NH, 128).T
    vecs[:, 241:257] = f(inputs["attn_sinks"])[0][None, :]
    out["vecs"] = vecs
    tabs = np.zeros((128, NTB), np.float32)
    tabs[:, 0:128] = np.eye(128, dtype=np.float32)
    s = np.arange(128)[:, None]
    t = np.arange(128)[None, :]
    tabs[:, 128:256] = ((s // 64 == t // 64) & (s <= t)).astype(np.float32)
    BIG = 1.0e6
    tabs[:, 256:384] = np.where(t < s, 128.0 + t - s, BIG)
    tabs[:, 384:512] = np.where(t >= s, (t - s).astype(np.float64), BIG)
    out["tabs"] = tabs
    return out


_NC_CACHE = {}


def kernel(**inputs):
    x = np.asarray(inputs["x"], dtype=np.float32)
    nb, s_, d_ = x.shape
    NT = s_ // T
    shared = host_layout(inputs, NT)
    if NT not in _NC_CACHE:
        _NC_CACHE[NT] = build(NT)
    nc = _NC_CACHE[NT]
    in_maps = []
    for b in range(nb):
        m = dict(shared)
        m["xT"] = np.ascontiguousarray(x[b].T).reshape(KC, 128, s_)
        in_maps.append(m)
    res = run_bass_kernel_spmd(nc, in_maps, core_ids=list(range(nb)))
    out = np.empty((nb, s_, d_), np.float32)
    for b in range(nb):
        out[b] = res.results[b]["yT"].reshape(D, s_).T
    return out
```

```python
import contextlib
import math
import numpy as np
import concourse.bass as bass
import concourse.mybir as mybir
from concourse.bass_utils import run_bass_kernel_spmd

F32 = mybir.dt.float32
BF16 = mybir.dt.bfloat16
AF = mybir.ActivationFunctionType
ALU = mybir.AluOpType

D = 1024
T = 512
KC = 8
NH = 8
DFF = 2816
NFC = 22
SEQ = 4096
EPS = 1e-6
NV = 257
NTB = 512
RING = 18432
PAGE = 1024
NWSEM = 8
LOOK = 3
USE_SCRATCH = True
HOIST = True
VCLOCK = True
HG_LEVEL = 9
DB2 = 3
CHAIN = 1
DELTA = 1
ENGS = ("pe", "dve", "act", "pool", "sp")


class V:
    __slots__ = ("ap", "res")

    def __init__(self, ap, res):
        self.ap = ap
        self.res = tuple(res)


class Sched:
    def __init__(self, nc):
        self.nc = nc
        self.ops = {e: [] for e in ENGS}
        self.last_w = {}
        self.readers = {}
        self.known = {e: {} for e in ENGS}
        self.dma_cnt = {}
        self.nseq = {e: 0 for e in ENGS}
        self.out_tokens = []
        self.gidx = 0
        self.tok_idx = {}
        self.clock = {}

    def _deps(self, eng, reads, writes):
        deps = {}

        def add(tok, same_ok):
            key, val, src = tok
            if src == eng and not same_ok:
                return
            if deps.get(key, 0) < val:
                deps[key] = val

        for r in reads:
            t = self.last_w.get(r)
            if t is not None:
                add(t, eng != "pe")
        for w in writes:
            t = self.last_w.get(w)
            if t is not None:
                add(t, eng != "pe")
            for key, (val, src) in self.readers.get(w, {}).items():
                add((key, val, src), eng != "pe")
        waits = []
        kn = self.known[eng]
        for key, val in sorted(deps.items(), key=lambda kv: -self.tok_idx.get(kv, 0)):
            if kn.get(key, 0) < val:
                waits.append((key, val))
                if VCLOCK:
                    for k2, v2 in self.clock.get((key, val), {}).items():
                        if kn.get(k2, 0) < v2:
                            kn[k2] = v2
                kn[key] = max(kn.get(key, 0), val)
        return waits

    def _commit(self, tok, reads, writes):
        key, val, src = tok
        for r in reads:
            d = self.readers.setdefault(r, {})
            if d.get(key, (0, None))[0] < val:
                d[key] = (val, src)
        for w in writes:
            self.last_w[w] = tok
            self.readers[w] = {}

    @staticmethod
    def _split(args, kw):
        reads, writes = [], []
        a2 = []
        for i, a in enumerate(args):
            if isinstance(a, V):
                (writes if i == 0 else reads).extend(a.res)
                a2.append(a.ap)
            else:
                a2.append(a)
        k2 = {}
        for k, a in kw.items():
            if isinstance(a, V):
                (writes if k in ("out", "accum_out") else reads).extend(a.res)
                k2[k] = a.ap
            else:
                k2[k] = a
        return a2, k2, reads, writes

    def op(self, eng, method, *args, xr=(), xw=(), **kw):
        a2, k2, reads, writes = self._split(args, kw)
        reads += list(xr)
        writes += list(xw)
        waits = self._deps(eng, reads, writes)
        self.nseq[eng] += 1
        tok = (eng, self.nseq[eng], eng)
        self._commit(tok, reads, writes)
        self.gidx += 1
        self.tok_idx[(eng, self.nseq[eng])] = self.gidx
        ck = dict(self.known[eng])
        ck[eng] = self.nseq[eng]
        self.clock[(eng, self.nseq[eng])] = ck
        self.ops[eng].append((waits, method, a2, k2, (eng, 1), self.gidx))
        return tok

    def dma(self, eng, out, in_, semkey, is_output=False, xw=(), **kw):
        a2, k2, reads, writes = self._split((out, in_), kw)
        writes += list(xw) + [("sem", semkey)]
        waits = self._deps(eng, reads, writes)
        self.dma_cnt[semkey] = self.dma_cnt.get(semkey, 0) + 16
        tok = (semkey, self.dma_cnt[semkey], "dma")
        self._commit(tok, reads, writes)
        self.gidx += 1
        self.tok_idx[(semkey, self.dma_cnt[semkey])] = self.gidx
        ck = dict(self.known[eng])
        ck[semkey] = self.dma_cnt[semkey]
        self.clock[(semkey, self.dma_cnt[semkey])] = ck
        self.ops[eng].append((waits, "dma_start", a2, k2, (semkey, 16), self.gidx))
        if is_output:
            self.out_tokens.append(tok)
        return tok

    def emit(self):
        nc = self.nc
        keys = set(ENGS) | set(self.dma_cnt.keys())
        keys = sorted(keys, key=str)
        with contextlib.ExitStack() as es:
            sems = {}
            for i, k in enumerate(keys):
                sems[k] = es.enter_context(nc.semaphore("s%d" % i))
            block = es.enter_context(nc.Block())
            fin = {}
            for key, val, _ in self.out_tokens:
                fin[key] = max(fin.get(key, 0), val)

            def plan(name):
                ops = self.ops[name]
                att = [None] * len(ops)
                alone = [[] for _ in ops]
                for i, (waits, method, a, k, inc, gi) in enumerate(ops):
                    if not waits:
                        continue
                    if method == "dma_start" or not HOIST:
                        alone[i] = list(waits)
                        continue
                    ws = sorted(waits, key=lambda w: -self.tok_idx.get((w[0], w[1]), 0))
                    att[i] = ws[0]
                    for w in ws[1:]:
                        ti = self.tok_idx.get((w[0], w[1]), 0)
                        placed = False
                        for i2 in range(i - 1, max(i - 24, -1), -1):
                            if ops[i2][5] <= ti:
                                break
                            if att[i2] is None and ops[i2][1] != "dma_start" and not alone[i2]:
                                att[i2] = w
                                placed = True
                                break
                        if not placed:
                            alone[i].append(w)
                return att, alone

            def run(engine, name):
                att, alone = plan(name)
                for i, (waits, method, a, k, (skey, amt), gi) in enumerate(self.ops[name]):
                    for wk, wv in alone[i]:
                        engine.wait_ge(sems[wk], wv)
                    inst = getattr(engine, method)(*a, **k)
                    if att[i] is not None:
                        inst._wait_ge(sems[att[i][0]], att[i][1])
                    inst.then_inc(sems[skey], amt)
                if name == "sp":
                    for key, val in fin.items():
                        engine.wait_ge(sems[key], val)

            @block.tensor
            def _(e):
                run(e, "pe")

            @block.vector
            def _(e):
                run(e, "dve")

            @block.scalar
            def _(e):
                run(e, "act")

            @block.gpsimd
            def _(e):
                run(e, "pool")

            @block.sync
            def _(e):
                run(e, "sp")


class Buf:
    def __init__(self, nc, name, shape, dtype, axis=1, g=None, psum=False):
        if psum:
            self.t = nc.alloc_psum_tensor("t_" + name, list(shape), dtype)
        else:
            self.t = nc.alloc_sbuf_tensor("t_" + name, list(shape), dtype)
        self.name = name
        self.shape = tuple(shape)
        self.axis = axis
        self.g = g if g is not None else shape[axis]

    def res(self, idx):
        if not isinstance(idx, tuple):
            idx = (idx,)
        n = self.shape[self.axis]
        if self.axis < len(idx):
            s = idx[self.axis]
            if isinstance(s, int):
                lo, hi = s, s + 1
            else:
                lo = 0 if s.start is None else s.start
                hi = n if s.stop is None else s.stop
        else:
            lo, hi = 0, n
        return [(self.name, i) for i in range(lo // self.g, (hi - 1) // self.g + 1)]

    def __getitem__(self, idx):
        return V(self.t[idx], self.res(idx))

    def allres(self):
        return self.res(())


def alibi_slopes():
    return [float(np.float32(2.0 ** (-8.0 * (h + 1) / 16))) for h in range(16)]


def build(NT=8, dump=False, phases=("hgrn", "ffn0", "attn", "ffn1")):
    S_ = NT * T
    nc = bass.Bass("TRN2", target_bir_lowering=False)
    S = Sched(nc)

    def din(name, shape):
        return nc.dram_tensor(name, list(shape), F32, kind="ExternalInput").ap()

    xT_d = din("xT", [KC, 128, S_])
    wi_d = din("w_i", [128, 8 * 1024])
    win_d = din("w_in", [NH, 128, 8 * 384])
    wout_d = din("w_out", [128, 8 * 1024])
    wup_d = din("w_up", [2, NFC, 128, 8 * 256])
    wdn_d = din("w_down", [2, KC, 128, NFC * 128])
    wkv_d = din("w_kv", [128, 8 * 512])
    wq_d = din("w_q", [128, 8 * 1024])
    wo_d = din("w_o", [128, 8 * 1024])
    vecs_d = din("vecs", [128, NV])
    tabs_d = din("tabs", [128, NTB])
    yT_d = nc.dram_tensor("yT", [KC, 128, S_], F32, kind="ExternalOutput").ap()
    dbg_d = None
    if dump:
        dbg_d = nc.dram_tensor("dbg", [4, KC, 128, S_], F32, kind="ExternalOutput").ap()

    B = lambda *a, **k: Buf(nc, *a, **k)
    hT = B("hT", [128, KC, T], F32, axis=1, g=1)
    xn = B("xn", [128, KC, T], BF16, axis=1, g=1)
    xsq = B("xsq", [128, 2, T], BF16, axis=1, g=1)
    rs = B("rs", [128, T], F32)
    rv = B("rv", [128, T], F32)
    cst = B("cst", [128, 8], F32)
    vtm = B("vtm", [128, 4, 1024], BF16, axis=1, g=1)
    mo = B("mo", [128, KC, T], BF16, axis=1, g=1)
    qs = B("qs", [128, T], F32)
    gsb = B("gsb", [128, 3, T], F32, axis=1, g=1)
    th = B("th", [128, T], F32)
    fg = B("fg", [128, T], F32)
    kk = B("kk", [128, T], F32)
    cp = B("cp", [128, T], F32)
    ri = B("ri", [128, T], F32)
    d1 = B("d1", [128, T], F32)
    qt = B("qt", [128, 3, T], BF16, axis=1, g=1)
    kt = B("kt", [128, 3, T], BF16, axis=1, g=1)
    kh = B("kh", [128, 3, T], BF16, axis=1, g=1)
    khT = B("khT", [128, 2, T], BF16, axis=1, g=1)
    Am = B("Am", [128, 2, T], BF16, axis=1, g=1)
    eb = B("eb", [128, 3, 8], F32, axis=1, g=1)
    osq = B("osq", [128, T], BF16)
    t1 = B("t1", [128, T], F32)
    S32 = B("S32", [128, NH, 128], F32, axis=1, g=1)
    SbP = B("SbP", [128, 2 * NH, 128], BF16, axis=1, g=1)
    Rr = B("Rr", [128, 16, 128], BF16, axis=1, g=1)
    hbuf = B("hbuf", [128, NFC, T], BF16, axis=1, g=1)
    G = B("G", [128, 2, T + 2], F32, axis=1, g=1)
    acc = B("acc", [128, 2, T], F32, axis=1, g=1)
    sl = B("sl", [128, 2, T], F32, axis=1, g=1)
    halo = B("halo", [128, 2 * NFC, 2], F32, axis=1, g=1)
    qT = B("qT", [128, KC, T], BF16, axis=1, g=1)
    kT = B("kT", [128, 2, 5 * 128], BF16, axis=2, g=128)
    vd = B("vd", [128, 5, 256], BF16, axis=1, g=1)
    scb = B("scb", [128, 2, 1024], F32, axis=1, g=1)
    pbuf = B("pbuf", [128, 2, 1024], BF16, axis=1, g=1)
    rec = B("rec", [128, 2, T], F32, axis=1, g=1)
    ost = B("ost", [128, 2, T], F32, axis=1, g=1)
    vecs = B("vecs", [128, NV], F32)
    tabs = B("tabs", [128, NTB], F32)
    identb = B("identb", [128, 128], BF16)
    onesb = B("onesb", [128, 128], BF16)
    lbt = B("lbt", [128, 8], F32)
    Ac = B("Ac", [128, 8], F32)
    Bc = B("Bc", [128, 8], F32)
    nBc = B("nBc", [128, 8], F32)
    esink = B("esink", [128, 16], F32)
    WR = B("WR", [128, RING], BF16, axis=1, g=PAGE)
    ps = [B("ps%d" % i, [128, 512], F32, axis=1, g=512, psum=True) for i in range(8)]

    slopes = alibi_slopes()

    wsched = []

    scr = {}

    def sdram(name, shape):
        return nc.dram_tensor(name, list(shape), BF16).ap()

    wi_s = sdram("s_w_i", [128, 8 * 1024])
    win_s = sdram("s_w_in", [NH, 128, 8 * 384])
    wout_s = sdram("s_w_out", [128, 8 * 1024])
    wup_s = sdram("s_w_up", [2, NFC, 128, 8 * 256])
    wdn_s = sdram("s_w_down", [2, KC, 128, NFC * 128])
    wkv_s = sdram("s_w_kv", [128, 8 * 512])
    wq_s = sdram("s_w_q", [128, 8 * 1024])
    wo_s = sdram("s_w_o", [128, 8 * 1024])
    scr_of = {id(wi_d): wi_s, id(wout_d): wout_s, id(wkv_d): wkv_s, id(wq_d): wq_s, id(wo_d): wo_s}

    def wadd(key, ap, n, sap=None):
        wsched.append((key, ap, n, sap))

    for j in range(NT):
        if "hgrn" in phases:
            wadd(("win", j, 0), win_d[0], 3072, win_s[0])
            wadd(("wi", j), wi_d, 8192, wi_s)
            for h in range(1, NH):
                wadd(("win", j, h), win_d[h], 3072, win_s[h])
            wadd(("wout", j), wout_d, 8192, wout_s)
        for l in range(2):
            if l == 1 and "attn" in phases:
                wadd(("wkv", j), wkv_d, 4096, wkv_s)
                wadd(("wq", j), wq_d, 8192, wq_s)
                wadd(("wo", j), wo_d, 8192, wo_s)
            if ("ffn%d" % l) in phases:
                for c in range(NFC):
                    wadd(("wup", j, l, c), wup_d[l, c], 2048, wup_s[l, c])
                for oc in range(KC):
                    wadd(("wdn", j, l, oc), wdn_d[l, oc], 2816, wdn_s[l, oc])
    widx = {k: i for i, (k, _, _, _) in enumerate(wsched)}
    wstate = {"next": 0, "head": 0, "live": [], "views": {}, "cnt": 0}

    def _try_alloc(n):
        live = wstate["live"]
        head = wstate["head"]
        if not live:
            wstate["head"] = n
            return 0
        tail = live[0][1]
        if head >= tail:
            if head + n <= RING:
                wstate["head"] = head + n
                return head
            if n < tail:
                wstate["head"] = n
                return 0
            return None
        if head + n < tail:
            wstate["head"] = head + n
            return head
        return None

    def _emit_load(i):
        key, ap, n, sap = wsched[i]
        off = _try_alloc(n)
        if off is None:
            return False
        wstate["live"].append((key, off, off + n))
        semkey = ("dw", wstate["cnt"] % NWSEM)
        wstate["cnt"] += 1
        jt = key[1]
        sres = [("scr",) + (key[0],) + tuple(key[2:])]
        if jt == 0 or not USE_SCRATCH:
            S.dma("pool", WR[:, off:off + n], V(ap, []), semkey)
            if USE_SCRATCH and NT > 1:
                S.dma("sp", V(sap, sres), WR[:, off:off + n], ("db", wstate["cnt"] % 4))
        else:
            S.dma("pool", WR[:, off:off + n], V(sap, sres), semkey)
        wstate["views"][key] = (off, n)
        return True

    def wget(key, inner):
        i = widx[key]
        while wstate["next"] <= min(i + LOOK, len(wsched) - 1):
            if not _emit_load(wstate["next"]):
                break
            wstate["next"] += 1
        assert key in wstate["views"], ("ring too small for", key)
        off, n = wstate["views"][key]
        res = WR.res((slice(None), slice(off, off + n)))
        view = WR.t[:, off:off + n].rearrange("p (k c) -> p k c", c=inner)
        return view, res

    def wdone(key):
        k0 = wstate["live"].pop(0)
        assert k0[0] == key, (k0, key)
        del wstate["views"][key]

    def vcol(c):
        return vecs[:, c:c + 1]

    def gain(gi, kc):
        return vcol(gi * 8 + kc)

    def cw(l, jj, c):
        return vcol(48 + (l * 3 + jj) * NFC + c)

    def cb(l, c):
        return vcol(180 + l * NFC + c)

    def stats_sq(kc):
        S.op("act", "activation", xsq[:, kc % 2, :], hT[:, kc, :], AF.Square)

    def stats_mm(kc):
        S.op("pe", "matmul", ps[7][:, :], onesb[:, :], xsq[:, kc % 2, :], start=(kc == 0), stop=(kc == KC - 1))

    def stats_fin():
        S.op("act", "activation", rv[:, :], ps[7][:, :], AF.Ln, bias=cst[:, 0:1], scale=1.0 / D)
        S.op("act", "activation", rs[:, :], rv[:, :], AF.Exp, scale=-0.5)

    def rms_stats():
        for kc in range(KC):
            stats_sq(kc)
            stats_mm(kc)
        stats_fin()

    def rms_apply(gi, dst=None):
        dst = xn if dst is None else dst
        for kc in range(KC):
            S.op("dve", "scalar_tensor_tensor", dst[:, kc, :], hT[:, kc, :], gain(gi, kc), rs[:, :], ALU.mult, ALU.mult)

    def proj_residual(key, src):
        wv, wres = wget(key, 1024)
        for oc in range(KC):
            pb = ps[oc % 4]
            for kc in range(KC):
                S.op("pe", "matmul", pb[:, :], V(wv[:, kc, oc * 128:(oc + 1) * 128], wres), src[:, kc, :],
                     start=(kc == 0), stop=(kc == KC - 1))
            if oc >= 1:
                stats_mm(oc - 1)
            S.op("dve", "tensor_tensor", hT[:, oc, :], hT[:, oc, :], pb[:, :], ALU.add)
            stats_sq(oc)
        stats_mm(KC - 1)
        stats_fin()
        wdone(key)

    def dump_h(j, stage):
        if dbg_d is None:
            return
        for kc in range(KC):
            S.dma("sp", V(dbg_d[stage, kc, :, j * T:(j + 1) * T], []), hT[:, kc, :], ("dd", kc), is_output=True)

    S.dma("sp", vecs[:, :], V(vecs_d, []), ("ds", 0))
    S.dma("sp", tabs[:, :], V(tabs_d, []), ("ds", 1))
    S.op("pool", "memset", onesb[:, :], 1.0)
    S.op("pool", "memset", cst[:, :], EPS)
    S.op("pool", "memset", halo[:, :, :], 0.0)
    S.op("pool", "memset", S32[:, :, :], 0.0)
    S.op("pool", "memset", SbP[:, :, :], 0.0)
    S.op("pool", "memset", kT[:, :, :], 0.0)
    S.op("pool", "memset", vd[:, :, :], 0.0)
    S.op("pool", "memset", d1[:, :], 0.0)
    S.op("dve", "tensor_copy", identb[:, :], tabs[:, 0:128])
    S.op("dve", "tensor_tensor", lbt[:, :], vecs[:, 225:233], vecs[:, 233:241], ALU.subtract)
    S.op("act", "activation", lbt[:, :], lbt[:, :], AF.Tanh, scale=0.5)
    S.op("dve", "tensor_scalar", Ac[:, :], lbt[:, :], 0.25, 0.75, ALU.mult, ALU.add)
    S.op("dve", "tensor_scalar", Bc[:, :], lbt[:, :], -0.25, 0.25, ALU.mult, ALU.add)
    S.op("dve", "tensor_scalar", nBc[:, :], lbt[:, :], 0.25, -0.25, ALU.mult, ALU.add)
    S.op("act", "activation", esink[:, :], vecs[:, 241:257], AF.Exp)

    amask = V(tabs.t[:, 128:256].rearrange("p (o c) -> p o c", o=1).to_broadcast([128, 4, 128]), tabs.allres())
    dtab = V(tabs.t[:, 256:512].rearrange("p (o c) -> p o c", o=1).to_broadcast([128, 2, 256]), tabs.allres())

    rs2 = B("rs2", [128, T], F32)

    def hgrn_iter(j, sX, sY, sZ):
        if sX is not None:
            hX, pX = sX, sX % 3
            wv, wres = wget(("win", j, hX), 384)
        if sY is not None:
            hY, pY3, pY2 = sY, sY % 3, sY % 2
            wrY = ((j + 1) % 2) * NH + hY
            pt3 = ps[3].t[:, :].bitcast(BF16)
            khTv = khT.t[:, pY2, :].rearrange("p (b k) -> p b k", k=128)
            AmvY = Am.t[:, pY2, :].rearrange("p (b k) -> p b k", k=128)
        if sZ is not None:
            hZ, pZ3, pZ2 = sZ, sZ % 3, sZ % 2
            rdZ = (j % 2) * NH + hZ
            AmvZ = Am.t[:, pZ2, :].rearrange("p (b k) -> p b k", k=128)
        def projX(idx):
            for kc in range(KC):
                S.op("pe", "matmul", ps[idx][:, :], V(wv[:, kc, idx * 128:(idx + 1) * 128], wres), xn[:, kc, :],
                     start=(kc == 0), stop=(kc == KC - 1))
        if sX is not None:
            projX(2)
            S.op("act", "activation", th[:, :], ps[2][:, :], AF.Tanh, scale=0.5)
            S.op("pool", "tensor_scalar", fg[:, :], th[:, :], Bc[:, hX:hX + 1], Ac[:, hX:hX + 1], ALU.mult, ALU.add)
            S.op("pool", "tensor_scalar", kk[:, :], th[:, :], nBc[:, hX:hX + 1], Bc[:, hX:hX + 1], ALU.mult, ALU.add)
            fgv = fg.t[:, :].rearrange("p (c j) -> p c j", j=64)
            d1v = d1.t[:, :].rearrange("p (c j) -> p c j", j=64)
            S.op("dve", "tensor_copy", V(d1v[:, :, 0:1], d1.allres()), V(fgv[:, :, 0:1], fg.allres()))
            S.op("dve", "memset", V(fgv[:, :, 0:1], fg.allres()), 0.0)
            S.op("dve", "tensor_tensor_scan", cp[:, :], fg[:, :], d1[:, :], 0.0, ALU.mult, ALU.add)
        if sY is not None:
            for pr in range(4):
                S.op("pe", "transpose", V(pt3[:, pr * 128:(pr + 1) * 128], ps[3].allres()), kh[:, pY3, pr * 128:(pr + 1) * 128], identb[:, :])
        if sZ is not None:
            for pr in range(4):
                S.op("pe", "matmul", ps[6][:, pr * 128:(pr + 1) * 128], vtm[:, pr, hZ * 128:(hZ + 1) * 128],
                     V(AmvZ[:, pr, :], Am.res((slice(None), pZ2))), start=True, stop=False)
                for half in range(2):
                    c = 2 * pr + half
                    sprev = SbP[:, rdZ, :] if c == 0 else Rr[:, pZ2 * 8 + c - 1, :]
                    S.op("pe", "matmul", ps[6][:, c * 64:(c + 1) * 64], sprev, qt[:, pZ3, c * 64:(c + 1) * 64],
                         start=False, stop=(half == 1))
        if sY is not None:
            S.op("act", "activation", khT[:, pY2, :], V(pt3[:, 0:512], ps[3].allres()), AF.Copy)
        if sZ is not None:
            S.op("act", "activation", osq[:, :], ps[6][:, :], AF.Square)
        if sY is not None:
            for c in range(8):
                blk, half = c // 2, c % 2
                dbank = ps[4] if c % 2 == 0 else ps[3]
                S.op("pe", "matmul", dbank[:, (c // 2) * 128:(c // 2 + 1) * 128],
                     V(khTv[half * 64:(half + 1) * 64, blk, :], khT.res((slice(None), pY2))),
                     vtm[half * 64:(half + 1) * 64, blk, hY * 128:(hY + 1) * 128], start=True, stop=True)
            for pr in range(4):
                S.op("pe", "matmul", ps[5][:, pr * 128:(pr + 1) * 128], kt[:, pY3, pr * 128:(pr + 1) * 128],
                     qt[:, pY3, pr * 128:(pr + 1) * 128], start=True, stop=True)
        if sZ is not None:
            S.op("pe", "matmul", ps[7][:, :], onesb[:, :], osq[:, :], start=True, stop=True)
        if sX is not None:
            projX(1)
            projX(0)
            wdone(("win", j, hX))
            S.op("act", "activation", ri[:, :], cp[:, :], AF.Ln)
            S.op("act", "activation", ri[:, :], ri[:, :], AF.Exp, scale=-1.0)
        if sY is not None:
            S.op("dve", "tensor_tensor", V(AmvY, Am.res((slice(None), pY2))),
                 V(ps[5].t[:, :].rearrange("p (b k) -> p b k", k=128), ps[5].allres()), amask, ALU.mult)
            for c in range(8):
                dbank = ps[4] if c % 2 == 0 else ps[3]
                S.op("dve", "scalar_tensor_tensor", S32[:, hY, :], S32[:, hY, :], eb[:, pY3, c:c + 1],
                     dbank[:, (c // 2) * 128:(c // 2 + 1) * 128], ALU.mult, ALU.add)
                dst = Rr[:, pY2 * 8 + c, :] if c < 7 else SbP[:, wrY, :]
                S.op("dve", "tensor_copy", dst, S32[:, hY, :])
        if sZ is not None:
            S.op("act", "activation", rs2[:, :], ps[7][:, :], AF.Ln, bias=cst[:, 0:1], scale=1.0 / 128)
            S.op("act", "activation", rs2[:, :], rs2[:, :], AF.Exp, scale=-0.5)
        if sX is not None:
            S.op("act", "activation", gsb[:, pX, :], ps[1][:, :], AF.Silu)
            S.op("act", "activation", qs[:, :], ps[0][:, :], AF.Silu)
            S.op("dve", "scalar_tensor_tensor", qt[:, pX, :], qs[:, :], 128 ** -0.5, cp[:, :], ALU.mult, ALU.mult)
            cpv = cp.t[:, :].rearrange("p (c j) -> p c j", j=64)
            S.op("dve", "tensor_copy", V(eb.t[:, pX, :].rearrange("p (c o) -> p c o", o=1), eb.res((slice(None), pX))),
                 V(cpv[:, :, 63:64], cp.allres()))
        if sX is not None:
            S.op("dve", "tensor_tensor", kt[:, pX, :], kk[:, :], ri[:, :], ALU.mult)
            ktv = kt.t[:, pX, :].rearrange("p (c j) -> p c j", j=64)
            khv = kh.t[:, pX, :].rearrange("p (c j) -> p c j", j=64)
            S.op("dve", "tensor_tensor", V(khv, kh.res((slice(None), pX))), V(ktv, kt.res((slice(None), pX))),
                 V(cpv[:, :, 63:64].to_broadcast([128, 8, 64]), cp.allres()), ALU.mult)
        if sZ is not None:
            S.op("dve", "tensor_tensor", t1[:, :], ps[6][:, :], rs2[:, :], ALU.mult)
            S.op("dve", "scalar_tensor_tensor", mo[:, hZ, :], t1[:, :], vcol(224), gsb[:, pZ3, :], ALU.mult, ALU.mult)

    def hgrn_layer(j):
        rms_apply(0)
        hgrn_iter(j, 0, None, None)
        wv, wres = wget(("wi", j), 1024)
        for blk in range(4):
            for half in range(2):
                n = blk * 2 + half
                pb = ps[3 + n % 2]
                for kc in range(KC):
                    S.op("pe", "matmul", pb[:, :], xn[:, kc, blk * 128:(blk + 1) * 128],
                         V(wv[:, kc, half * 512:(half + 1) * 512], wres), start=(kc == 0), stop=(kc == KC - 1))
                S.op("act", "activation", vtm[:, blk, half * 512:(half + 1) * 512], pb[:, :], AF.Copy)
        wdone(("wi", j))
        for step in range(1, NH + 2):
            hgrn_iter(j, step if step < NH else None, step - 1 if 0 <= step - 1 < NH else None,
                      step - 2 if 0 <= step - 2 < NH else None)
        proj_residual(("wout", j), mo)

    def ffn(j, l):
        rms_apply(1 if l == 0 else 4)

        def stage1_pe(cs):
            ws = [wget(("wup", j, l, c), 256) for c in cs]
            if len(cs) == 1:
                order = [(0, half, kc) for half in range(2) for kc in range(KC)]
            else:
                order = [(i, half, kc) for kc in range(KC) for i in range(len(cs)) for half in range(2)]
            for i, half, kc in order:
                c = cs[i]
                wv, wres = ws[i]
                pb = ps[c % 2] if half == 0 else ps[2 + c % 5]
                S.op("pe", "matmul", pb[:, :], V(wv[:, kc, half * 128:(half + 1) * 128], wres), xn[:, kc, :],
                     start=(kc == 0), stop=(kc == KC - 1))
            for c in cs:
                wdone(("wup", j, l, c))

        def stage1(c, pe=True):
            if pe:
                stage1_pe([c])
            b = c % 2
            pg, pv = ps[b], ps[2 + c % 5]
            hl = l * NFC + c
            S.op("dve", "tensor_copy", G[:, b, 0:2], halo[:, hl, :])
            S.op("act", "activation", G[:, b, 2:T + 2], pg[:, :], AF.Copy)
            S.op("act", "activation", acc[:, b, :], pg[:, :], AF.Identity, bias=cb(l, c), scale=cw(l, 2, c))
            S.op("dve", "tensor_copy", halo[:, hl, :], G[:, b, T:T + 2])

        def stage2(c):
            b = c % 2
            pv = ps[2 + c % 5]
            S.op("dve", "scalar_tensor_tensor", acc[:, b, :], G[:, b, 1:T + 1], cw(l, 1, c), acc[:, b, :], ALU.mult, ALU.add)
            S.op("dve", "scalar_tensor_tensor", acc[:, b, :], G[:, b, 0:T], cw(l, 0, c), acc[:, b, :], ALU.mult, ALU.add)
            S.op("act", "activation", sl[:, b, :], acc[:, b, :], AF.Silu)
            S.op("dve", "tensor_tensor", hbuf[:, c, :], sl[:, b, :], pv[:, :], ALU.mult)

        stage1_pe([0, 1])
        stage1(0, pe=False)
        stage1(1, pe=False)
        stage2(0)
        for c in range(2, NFC + 1):
            if c < NFC:
                stage1(c)
            stage2(c - 1)
        for oc in range(KC):
            wv, wres = wget(("wdn", j, l, oc), 128)
            pb = ps[oc % 2]
            for c in range(NFC):
                S.op("pe", "matmul", pb[:, :], V(wv[:, c, :], wres), hbuf[:, c, :], start=(c == 0), stop=(c == NFC - 1))
            if oc >= 1:
                stats_mm(oc - 1)
            S.op("dve", "tensor_tensor", hT[:, oc, :], hT[:, oc, :], pb[:, :], ALU.add)
            stats_sq(oc)
            wdone(("wdn", j, l, oc))
        stats_mm(KC - 1)
        stats_fin()

    def attn_layer(j):
        rms_apply(2, dst=mo)
        rms_apply(3)
        wv, wres = wget(("wkv", j), 512)
        for g in range(2):
            for kc in range(KC):
                S.op("pe", "matmul", ps[g][:, :], V(wv[:, kc, g * 128:(g + 1) * 128], wres), mo[:, kc, :],
                     start=(kc == 0), stop=(kc == KC - 1))
            S.op("act", "activation", kT[:, g, 128:640], ps[g][:, :], AF.Copy)
        for blk in range(4):
            pb = ps[2 + blk % 2]
            for kc in range(KC):
                S.op("pe", "matmul", pb[:, 0:256], mo[:, kc, blk * 128:(blk + 1) * 128], V(wv[:, kc, 256:512], wres),
                     start=(kc == 0), stop=(kc == KC - 1))
            S.op("act", "activation", vd[:, 1 + blk, :], pb[:, 0:256], AF.Copy)
        wdone(("wkv", j))
        wv, wres = wget(("wq", j), 1024)
        for qc in range(KC):
            pb = ps[4 + qc % 2]
            for kc in range(KC):
                S.op("pe", "matmul", pb[:, :], V(wv[:, kc, qc * 128:(qc + 1) * 128], wres), xn[:, kc, :],
                     start=(kc == 0), stop=(kc == KC - 1))
            S.op("act", "activation", qT[:, qc, :], pb[:, :], AF.Copy)
        wdone(("wq", j))
        dtab_r = [V(tabs.t[:, 256 + r * 128:256 + (r + 1) * 128].rearrange("p (o c) -> p o c", o=1).to_broadcast([128, 4, 128]),
                    tabs.allres()) for r in range(2)]

        def s1(hh):
            g, qc, po, b = hh // 8, hh // 2, (hh % 2) * 64, hh % 2
            for r in range(2):
                bank = ps[2 * b + r]
                for qb in range(4):
                    slot = qb + r
                    S.op("pe", "matmul", bank[:, qb * 128:(qb + 1) * 128],
                         kT[po:po + 64, g, slot * 128:(slot + 1) * 128], qT[po:po + 64, qc, qb * 128:(qb + 1) * 128],
                         start=True, stop=True)
            for r in range(2):
                bank = ps[2 * b + r]
                S.op("dve", "scalar_tensor_tensor",
                     V(scb.t[:, b, r * 512:(r + 1) * 512].rearrange("p (q c) -> p q c", c=128), scb.res((slice(None), b))),
                     dtab_r[r], -8.0 * slopes[hh],
                     V(bank.t[:, :].rearrange("p (q c) -> p q c", c=128), bank.allres()), ALU.mult, ALU.add)
            S.op("act", "activation", pbuf[:, b, :], scb[:, b, :], AF.Exp, scale=0.125)

        def s2(hh):
            g, qc, po, b = hh // 8, hh // 2, (hh % 2) * 64, hh % 2
            PV, DN = ps[4 + b], ps[6 + b]
            pres = pbuf.res((slice(None), b))
            for qb in range(4):
                first = True
                for r in range(2):
                    if j == 0 and qb == 0 and r == 0:
                        continue
                    slot = qb + r
                    n = r * 4 + qb
                    S.op("pe", "matmul", PV[:, qb * 128:(qb + 1) * 128], vd[:, slot, g * 128:(g + 1) * 128],
                         pbuf[:, b, n * 128:(n + 1) * 128], start=first, stop=(r == 1))
                    first = False
            S.op("pe", "matmul", DN[:, :], onesb[:, :], pbuf[:, b, 512:1024], start=True, stop=False)
            if j == 0:
                S.op("pe", "matmul", DN[:, 128:512], onesb[:, :], pbuf[:, b, 128:512], start=False, stop=True)
            else:
                S.op("pe", "matmul", DN[:, :], onesb[:, :], pbuf[:, b, 0:512], start=False, stop=True)

        def s3a(hh):
            po, b = (hh % 2) * 64, hh % 2
            DN = ps[6 + b]
            S.op("act", "activation", rec[po:po + 64, b, :], DN[po:po + 64, :], AF.Ln, bias=esink[po:po + 64, hh:hh + 1])
            S.op("act", "activation", rec[po:po + 64, b, :], rec[po:po + 64, b, :], AF.Exp, scale=-1.0)

        def s3d(hh):
            qc, po, b = hh // 2, (hh % 2) * 64, hh % 2
            PV = ps[4 + b]
            S.op("dve", "tensor_tensor", mo[po:po + 64, qc, :], PV[po:po + 64, :], rec[po:po + 64, b, :], ALU.mult)

        s1(0)
        for hh in range(17):
            if hh >= 1:
                s3a(hh - 1)
            if hh + 1 < 16:
                s1(hh + 1)
            if hh < 16:
                s2(hh)
            if hh >= 1:
                s3d(hh - 1)
        for g in range(2):
            S.op("act", "activation", kT[:, g, 0:128], kT[:, g, 512:640], AF.Copy)
        S.op("act", "activation", vd[:, 0, :], vd[:, 4, :], AF.Copy)
        proj_residual(("wo", j), mo)

    for j in range(NT):
        for kc in range(KC):
            S.dma("sp", hT[:, kc, :], V(xT_d[kc, :, j * T:(j + 1) * T], []), ("dx", kc))
        rms_stats()
        if "hgrn" in phases:
            hgrn_layer(j)
        dump_h(j, 0)
        if "ffn0" in phases:
            ffn(j, 0)
        dump_h(j, 1)
        if "attn" in phases:
            attn_layer(j)
        dump_h(j, 2)
        if "ffn1" in phases:
            ffn(j, 1)
        dump_h(j, 3)
        for kc in range(KC):
            b = kc % 2
            S.op("dve", "scalar_tensor_tensor", ost[:, b, :], hT[:, kc, :], gain(5, kc), rs[:, :], ALU.mult, ALU.mult)
            S.dma("sp", V(yT_d[kc, :, j * T:(j + 1) * T], []), ost[:, b, :], ("dy", b), is_output=True)
    S.emit()
    return nc


def _chunked(w):
    n = w.shape[1]
    return np.ascontiguousarray(w.reshape(KC, 128, n).transpose(1, 0, 2)).reshape(128, KC * n)


def host_layout(inputs, NT=8):
    f = lambda a: np.ascontiguousarray(np.asarray(a, dtype=np.float32))
    hg_w_in = f(inputs["hg_w_in"])[0]
    out = {}
    out["w_i"] = _chunked(hg_w_in[:, 2048:3072])
    w_in = np.empty((NH, 128, KC * 384), np.float32)
    for h in range(NH):
        cols = np.concatenate([hg_w_in[:, h * 128:(h + 1) * 128],
                               hg_w_in[:, 3072 + h * 128:3072 + (h + 1) * 128],
                               hg_w_in[:, 1024 + h * 128:1024 + (h + 1) * 128]], axis=1)
        w_in[h] = _chunked(cols)
    out["w_in"] = w_in
    out["w_out"] = _chunked(f(inputs["hg_w_out"])[0])
    w_up_in = f(inputs["ffn_w_up"])
    w_up = np.empty((2, NFC, 128, KC * 256), np.float32)
    for l in range(2):
        for c in range(NFC):
            cols = np.concatenate([w_up_in[l][:, c * 128:(c + 1) * 128],
                                   w_up_in[l][:, DFF + c * 128:DFF + (c + 1) * 128]], axis=1)
            w_up[l, c] = _chunked(cols)
    out["w_up"] = w_up
    w_dn_in = f(inputs["ffn_w_down"])
    w_dn = np.empty((2, KC, 128, NFC * 128), np.float32)
    for l in range(2):
        wd = w_dn_in[l].reshape(NFC, 128, KC, 128)
        w_dn[l] = wd.transpose(2, 1, 0, 3).reshape(KC, 128, NFC * 128)
    out["w_down"] = w_dn
    w_kv = f(inputs["w_kv"])
    k0, k1, v0, v1 = w_kv[:, 0:64], w_kv[:, 64:128], w_kv[:, 128:192], w_kv[:, 192:256]
    out["w_kv"] = _chunked(np.concatenate([k0, k0, k1, k1, v0, v0, v1, v1], axis=1))
    out["w_q"] = _chunked(f(inputs["attn_w_q"])[0])
    out["w_o"] = _chunked(f(inputs["attn_w_o"])[0])
    vecs = np.zeros((128, NV), np.float32)
    gl = [inputs["hg_norm"][0], inputs["ffn_norm"][0], inputs["kv_norm"], inputs["attn_norm"][0],
          inputs["ffn_norm"][1], inputs["final_norm"]]
    for gi, gvec in enumerate(gl):
        vecs[:, gi * 8:(gi + 1) * 8] = f(gvec).reshape(KC, 128).T
    cwv = f(inputs["ffn_conv_w"])
    cbv = f(inputs["ffn_conv_b"])
    for l in range(2):
        for jj in range(3):
            vecs[:, 48 + (l * 3 + jj) * NFC:48 + (l * 3 + jj + 1) * NFC] = cwv[l, jj].reshape(NFC, 128).T
        vecs[:, 180 + l * NFC:180 + (l + 1) * NFC] = cbv[l].reshape(NFC, 128).T
    vecs[:, 224] = f(inputs["hg_out_norm"])[0]
    lbl = f(inputs["hg_lb_logits"])
    for r in range(2):
        vecs[:, 225 + r * 8:225 + (r + 1) * 8] = lbl[r].reshape(NH, 128).T
    vecs[:, 241:257] = f(inputs["attn_sinks"])[0][None, :]
    out["vecs"] = vecs
    tabs = np.zeros((128, NTB), np.float32)
    tabs[:, 0:128] = np.eye(128, dtype=np.float32)
    s = np.arange(128)[:, None]
    t = np.arange(128)[None, :]
    tabs[:, 128:256] = ((s // 64 == t // 64) & (s <= t)).astype(np.float32)
    BIG = 1.0e6
    tabs[:, 256:384] = np.where(t < s, 128.0 + t - s, BIG)
    tabs[:, 384:512] = np.where(t >= s, (t - s).astype(np.float64), BIG)
    out["tabs"] = tabs
    return out


_NC_CACHE = {}


def kernel(**inputs):
    x = np.asarray(inputs["x"], dtype=np.float32)
    nb, s_, d_ = x.shape
    NT = s_ // T
    shared = host_layout(inputs, NT)
    if NT not in _NC_CACHE:
        _NC_CACHE[NT] = build(NT)
    nc = _NC_CACHE[NT]
    in_maps = []
    for b in range(nb):
        m = dict(shared)
        m["xT"] = np.ascontiguousarray(x[b].T).reshape(KC, 128, s_)
        in_maps.append(m)
    res = run_bass_kernel_spmd(nc, in_maps, core_ids=list(range(nb)))
    out = np.empty((nb, s_, d_), np.float32)
    for b in range(nb):
        out[b] = res.results[b]["yT"].reshape(D, s_).T
    return out
```
